# Optimizing a Trainium2 kernel written in Bass

```python
import math
import jax
import jax.numpy as jnp
from jax import lax
import numpy as np

D_MODEL = 2048
BATCH = 8
SEQ = 2048
DEPTH = 2

BRANCH_WIDTH = D_MODEL // 2
MIX_WIDTH = 3 * BRANCH_WIDTH
NORM_EPS = 1e-6

A_QK_DIM = 64
A_V_DIM = 2 * A_QK_DIM
A_HEADS = BRANCH_WIDTH // A_V_DIM
A_SUBLN_EPS = 1e-5
Q_BLOCK = 128
NUM_BUCKETS = 32
MAX_DISTANCE = 128

B_HEAD_SIZE = 64
B_HEADS = BRANCH_WIDTH // B_HEAD_SIZE
B_LORA = 64
B_LNX_EPS = 64e-5
B_SHIFT_WIDTH = 3 * BRANCH_WIDTH + 4 * B_LORA

C_DK = 128
C_DV = 128
C_HEADS = BRANCH_WIDTH // C_DV
C_CHUNK = 32
LB_FLOOR = 1e-30

PROJ_SIZES = (
    A_HEADS * 2 * A_QK_DIM,
    A_HEADS * 2 * A_QK_DIM,
    A_HEADS * A_V_DIM,
    BRANCH_WIDTH,
    B_SHIFT_WIDTH,
    BRANCH_WIDTH,
    C_HEADS * C_DK,
    C_HEADS * C_DV,
    2 * C_HEADS * C_DK,
    BRANCH_WIDTH,
)
PROJ_WIDTH = sum(PROJ_SIZES)

kernel_name = "hybrid_diffattn_rwkv7_hgrn2_encoder"


def _split(t, sizes):
    parts, start = [], 0
    for size in sizes:
        parts.append(t[..., start:start + size])
        start += size
    return parts


def rmsnorm(x, g, eps=NORM_EPS):
    xf = x.astype(jnp.float32)
    y = xf * lax.rsqrt(jnp.mean(xf * xf, axis=-1, keepdims=True) + eps)
    return (y * g.astype(jnp.float32)).astype(x.dtype)


def t5_bucket(rel):
    half = NUM_BUCKETS // 2
    max_exact = half // 2
    n = jnp.abs(rel)
    nf = jnp.maximum(n, max_exact).astype(jnp.float32)
    large = max_exact + (jnp.log(nf / max_exact) / math.log(MAX_DISTANCE / max_exact)
                         * (half - max_exact)).astype(jnp.int32)
    large = jnp.minimum(large, half - 1)
    return jnp.where(rel > 0, half, 0) + jnp.where(n < max_exact, n, large)


def diff_attention(q, k, v, rel_bias, lam_q1, lam_k1, lam_q2, lam_k2, subln_g, layer_idx):
    f32 = jnp.float32
    bsz, seq = q.shape[0], q.shape[1]
    n_blocks = seq // Q_BLOCK
    lambda_init = 0.8 - 0.6 * math.exp(-0.3 * layer_idx)
    lam = (jnp.exp(jnp.sum(lam_q1.astype(f32) * lam_k1.astype(f32)))
           - jnp.exp(jnp.sum(lam_q2.astype(f32) * lam_k2.astype(f32))) + lambda_init)
    scale = A_QK_DIM ** -0.5
    key_pos = jnp.arange(seq, dtype=jnp.int32)
    q_blocks = q.reshape(bsz, n_blocks, Q_BLOCK, A_HEADS, 2, A_QK_DIM).swapaxes(0, 1)

    def one_block(args):
        q_blk, blk = args
        q_pos = blk * Q_BLOCK + jnp.arange(Q_BLOCK, dtype=jnp.int32)
        bias = rel_bias[t5_bucket(key_pos[None, :] - q_pos[:, None])]
        bias = jnp.transpose(bias, (2, 0, 1)).astype(f32)
        logits = jnp.einsum('bqhmd,bkhmd->bhmqk', q_blk, k).astype(f32) * scale
        probs = jax.nn.softmax(logits + bias[None, :, None], axis=-1)
        weights = probs[:, :, 0] - lam * probs[:, :, 1]
        return jnp.einsum('bhqk,bkhd->bqhd', weights.astype(v.dtype), v)

    out = lax.map(one_block, (q_blocks, jnp.arange(n_blocks, dtype=jnp.int32)))
    out = out.swapaxes(0, 1).reshape(bsz, seq, A_HEADS, A_V_DIM)
    out = rmsnorm(out, subln_g, eps=A_SUBLN_EPS) * (1.0 - lambda_init)
    return out.reshape(bsz, seq, A_HEADS * A_V_DIM)


def centred_shift(p, mu):
    prev = jnp.pad(p, ((0, 0), (1, 0), (0, 0)))[:, :-1]
    nxt = jnp.pad(p, ((0, 0), (0, 1), (0, 0)))[:, 1:]
    return p + mu[0] * (prev - p) + mu[1] * (nxt - p)


def _directional(t):
    bsz, seq = t.shape[0], t.shape[1]
    both = jnp.stack([t[:, :, 0], jnp.flip(t[:, :, 1], axis=1)], axis=0)
    return both.reshape(2, bsz, seq, B_HEADS, B_HEAD_SIZE).transpose(2, 0, 1, 3, 4)


def rwkv7_step(state, inp):
    r, w, kk, kk_a, v, k = inp
    sa = jnp.einsum('dbhij,dbhj->dbhi', state, -kk)
    state = (state * w[..., None, :] + sa[..., :, None] * kk_a[..., None, :]
             + v[..., :, None] * k[..., None, :])
    return state, jnp.einsum('dbhij,dbhj->dbhi', state, r)


def rwkv7_bidir(r, k, v, w_down, a_down, w0, w_up, a0, a_up, k_k, k_a, r_k, lnx_g, lnx_b):
    f32 = jnp.float32
    bsz, seq, width = r.shape
    r, k, v = r.astype(f32), k.astype(f32), v.astype(f32)
    decay = jnp.exp(-math.exp(-0.5) * jax.nn.sigmoid(
        w0.astype(f32) + jnp.einsum('bsdl,dlc->bsdc', jnp.tanh(w_down.astype(f32)), w_up.astype(f32))))
    a = jax.nn.sigmoid(a0.astype(f32) + jnp.einsum('bsdl,dlc->bsdc', a_down.astype(f32), a_up.astype(f32)))
    kk = (k * k_k.astype(f32)).reshape(bsz, seq, B_HEADS, B_HEAD_SIZE)
    kk = kk / jnp.maximum(jnp.linalg.norm(kk, axis=-1, keepdims=True), 1e-12)
    kk = kk.reshape(bsz, seq, 1, width)
    k_mod = k[:, :, None] * (1.0 + (a - 1.0) * k_a.astype(f32))
    both = lambda t: jnp.broadcast_to(t, (bsz, seq, 2, width))
    xs = (_directional(both(r[:, :, None])), _directional(decay), _directional(both(kk)),
          _directional(kk * a), _directional(both(v[:, :, None])), _directional(k_mod))
    state0 = jnp.zeros((2, bsz, B_HEADS, B_HEAD_SIZE, B_HEAD_SIZE), f32)
    _, ys = lax.scan(rwkv7_step, state0, xs)
    y = (ys[:, 0] + jnp.flip(ys[:, 1], axis=0)).transpose(1, 0, 2, 3)
    mu = jnp.mean(y, axis=-1, keepdims=True)
    var = jnp.mean(jnp.square(y - mu), axis=-1, keepdims=True)
    y = ((y - mu) * lax.rsqrt(var + B_LNX_EPS)).reshape(bsz, seq, width)
    y = y * lnx_g.astype(f32) + lnx_b.astype(f32)
    bonus = jnp.einsum('bshn,bsdhn,hn->bsh', r.reshape(bsz, seq, B_HEADS, B_HEAD_SIZE),
                       k_mod.reshape(bsz, seq, 2, B_HEADS, B_HEAD_SIZE),
                       r_k.astype(f32).reshape(B_HEADS, B_HEAD_SIZE))
    return y + (bonus[..., None] * v.reshape(bsz, seq, B_HEADS, B_HEAD_SIZE)).reshape(bsz, seq, width)


def gla_chunkwise(q, k, v, log_f):
    nbat, heads, seq, dk = q.shape
    dv = v.shape[-1]
    n_chunks = seq // C_CHUNK
    to_chunks = lambda t: t.reshape(nbat, heads, n_chunks, C_CHUNK, t.shape[-1]).transpose(2, 0, 1, 3, 4)
    mask = jnp.tril(jnp.ones((C_CHUNK, C_CHUNK), dtype=bool))[:, :, None]

    def step(state, inp):
        q_c, k_c, v_c, lf_c = inp
        cum = jnp.cumsum(lf_c, axis=-2)
        diff = cum[..., :, None, :] - cum[..., None, :, :]
        pair_decay = jnp.where(mask, jnp.exp(jnp.where(mask, diff, 0.0)), 0.0)
        scores = jnp.einsum('bhtk,bhsk,bhtsk->bhts', q_c, k_c, pair_decay)
        o_c = (jnp.einsum('bhts,bhsv->bhtv', scores, v_c)
               + jnp.einsum('bhtk,bhkv->bhtv', q_c * jnp.exp(cum), state))
        last = cum[..., -1:, :]
        state = (state * jnp.exp(last)[..., 0, :, None]
                 + jnp.einsum('bhsk,bhsv->bhkv', k_c * jnp.exp(last - cum), v_c))
        return state, o_c

    state0 = jnp.zeros((nbat, heads, dk, dv), q.dtype)
    _, o = lax.scan(step, state0, (to_chunks(q), to_chunks(k), to_chunks(v), to_chunks(log_f)))
    return o.transpose(1, 2, 0, 3, 4).reshape(nbat, heads, seq, dv)


def hgrn2_bidir(q, i, f_logits, lower_bound, norm_g):
    f32 = jnp.float32
    bsz, seq = q.shape[0], q.shape[1]
    z = f_logits.astype(f32)
    log_sig = jax.nn.log_sigmoid(z)
    log_lb = jnp.log(jnp.maximum(lower_bound, LB_FLOOR))
    log_f = jnp.where(lower_bound > 0.0,
                      jnp.logaddexp(log_lb, jnp.log1p(-lower_bound) + log_sig), log_sig)
    k = (1.0 - lower_bound) * jax.nn.sigmoid(-z)
    heads = lambda t, d: t.reshape(t.shape[0], seq, C_HEADS, d).transpose(0, 2, 1, 3)
    both = lambda t_f, t_b: jnp.concatenate([t_f, jnp.flip(t_b, axis=1)], axis=0)
    qf, vf = q.astype(f32), i.astype(f32)
    o = gla_chunkwise(heads(both(qf, qf), C_DK), heads(both(k[:, :, 0], k[:, :, 1]), C_DK),
                      heads(both(vf, vf), C_DV), heads(both(log_f[:, :, 0], log_f[:, :, 1]), C_DK))
    o = o.transpose(0, 2, 1, 3)
    o = o[:bsz] + jnp.flip(o[bsz:], axis=1)
    return rmsnorm(o, norm_g).reshape(bsz, seq, C_HEADS * C_DV)


def setup_inputs(seed: int = 0) -> dict:
    key = jax.random.key(seed)
    ks = iter(jax.random.split(key, 24))
    f32 = jnp.float32
    nrm = lambda shape, scale: scale * jax.random.normal(next(ks), shape, f32)
    gain = lambda shape: 1.0 + 0.05 * jax.random.normal(next(ks), shape, f32)
    return {
        "x": nrm((BATCH, SEQ, D_MODEL), 1.0),
        "rel_bias": nrm((NUM_BUCKETS, A_HEADS), 0.5),
        "pre_norm_g": gain((DEPTH, D_MODEL)),
        "post_norm_g": gain((DEPTH, D_MODEL)),
        "w_in": nrm((DEPTH, D_MODEL, PROJ_WIDTH), D_MODEL ** -0.5),
        "w_out": nrm((DEPTH, MIX_WIDTH, D_MODEL), MIX_WIDTH ** -0.5),
        "lambda_q1": nrm((DEPTH, A_QK_DIM), 0.1),
        "lambda_k1": nrm((DEPTH, A_QK_DIM), 0.1),
        "lambda_q2": nrm((DEPTH, A_QK_DIM), 0.1),
        "lambda_k2": nrm((DEPTH, A_QK_DIM), 0.1),
        "subln_g": gain((DEPTH, A_V_DIM)),
        "rwkv_shift_mu": jax.random.uniform(next(ks), (DEPTH, 2, B_SHIFT_WIDTH), f32, 0.0, 0.5),
        "rwkv_w0": jax.random.uniform(next(ks), (DEPTH, 2, BRANCH_WIDTH), f32, -2.0, 2.0),
        "rwkv_w_up": nrm((DEPTH, 2, B_LORA, BRANCH_WIDTH), 0.1),
        "rwkv_a0": nrm((DEPTH, 2, BRANCH_WIDTH), 0.5),
        "rwkv_a_up": nrm((DEPTH, 2, B_LORA, BRANCH_WIDTH), B_LORA ** -0.5),
        "rwkv_k_k": 0.85 + 0.05 * jax.random.normal(next(ks), (DEPTH, BRANCH_WIDTH), f32),
        "rwkv_k_a": gain((DEPTH, BRANCH_WIDTH)),
        "rwkv_r_k": nrm((DEPTH, BRANCH_WIDTH), 0.1),
        "rwkv_lnx_g": gain((DEPTH, BRANCH_WIDTH)),
        "rwkv_lnx_b": nrm((DEPTH, BRANCH_WIDTH), 0.01),
        "hgrn_lb_logits": nrm((2, DEPTH, C_HEADS * C_DK), 0.5),
        "hgrn_norm_g": gain((DEPTH, C_DV)),
    }


def reference(x, rel_bias, pre_norm_g, post_norm_g, w_in, w_out, lambda_q1, lambda_k1, lambda_q2,
              lambda_k2, subln_g, rwkv_shift_mu, rwkv_w0, rwkv_w_up, rwkv_a0, rwkv_a_up, rwkv_k_k,
              rwkv_k_a, rwkv_r_k, rwkv_lnx_g, rwkv_lnx_b, hgrn_lb_logits, hgrn_norm_g):
    bsz, seq, _ = x.shape
    lb = jax.nn.softmax(hgrn_lb_logits.astype(jnp.float32), axis=1)
    lb = jnp.cumsum(lb, axis=1) - lb[:, :1]
    for l in range(DEPTH):
        u = rmsnorm(x, pre_norm_g[l])
        proj = jnp.einsum('bsd,dp->bsp', u, w_in[l])
        aq, ak, av, ag, b_streams, bg, cq, ci, cf, cg = _split(proj, PROJ_SIZES)
        y_a = diff_attention(
            aq.reshape(bsz, seq, A_HEADS, 2, A_QK_DIM), ak.reshape(bsz, seq, A_HEADS, 2, A_QK_DIM),
            av.reshape(bsz, seq, A_HEADS, A_V_DIM), rel_bias, lambda_q1[l], lambda_k1[l],
            lambda_q2[l], lambda_k2[l], subln_g[l], l)
        br, bk, bv, bw, ba = _split(centred_shift(b_streams, rwkv_shift_mu[l]),
                                    (BRANCH_WIDTH, BRANCH_WIDTH, BRANCH_WIDTH, 2 * B_LORA, 2 * B_LORA))
        y_b = rwkv7_bidir(br, bk, bv, bw.reshape(bsz, seq, 2, B_LORA), ba.reshape(bsz, seq, 2, B_LORA),
                          rwkv_w0[l], rwkv_w_up[l], rwkv_a0[l], rwkv_a_up[l], rwkv_k_k[l], rwkv_k_a[l],
                          rwkv_r_k[l], rwkv_lnx_g[l], rwkv_lnx_b[l])
        y_c = hgrn2_bidir(cq, ci, cf.reshape(bsz, seq, 2, C_HEADS * C_DK), lb[:, l], hgrn_norm_g[l])
        mixed = jnp.concatenate([(y_a * jax.nn.silu(ag)).astype(u.dtype),
                                 (y_b * jax.nn.silu(bg)).astype(u.dtype),
                                 (y_c * jax.nn.silu(cg)).astype(u.dtype)], axis=-1)
        x = x + rmsnorm(jnp.einsum('bsm,md->bsd', mixed, w_out[l]), post_norm_g[l])
    return x
```

```python
import math
from contextlib import ExitStack
import numpy as np
import concourse.bass as bass
import concourse.mybir as mybir
from concourse.bass_utils import run_bass_kernel_spmd

F32 = mybir.dt.float32
BF16 = mybir.dt.bfloat16
AF = mybir.ActivationFunctionType
ALU = mybir.AluOpType
AX = mybir.AxisListType

S = 2048
D = 2048
PW = 13568
MW = 3072
NL = 2
EPS = 1e-6

ENGS = ["sync", "scalar", "vector", "gpsimd", "tensor"]
SEM_CAP = 30000
N_DMA_SEMS = 24


class Prog:
    def __init__(self, nc, es):
        self.nc = nc
        self.es = es
        self.q = {e: [] for e in ENGS}
        self.cur = {}
        self.waited = {e: {} for e in ENGS}
        self.lw = {}
        self.rd = {}
        self.nsem = 0
        self.dma_sems = []
        self.dma_state = []
        self.dma_rr = 0
        self.n_ins = {e: 0 for e in ENGS}
        for e in ["scalar", "vector", "gpsimd", "tensor"]:
            self._new_phase(e)
        for i in range(N_DMA_SEMS):
            s = es.enter_context(nc.semaphore(f"dma{i}"))
            self.dma_sems.append(s)
            self.dma_state.append(0)

    def _new_phase(self, e):
        s = self.es.enter_context(self.nc.semaphore(f"s_{e}_{self.nsem}"))
        self.nsem += 1
        self.cur[e] = [s, 0]

    def _need(self, eng, deps):
        w = self.waited[eng]
        best = {}
        for (s, v) in deps:
            if v <= w.get(id(s), 0):
                continue
            if v > best.get(id(s), (None, 0))[1]:
                best[id(s)] = (s, v)
        for sid, (s, v) in best.items():
            w[sid] = v
            self.q[eng].append(lambda E, s=s, v=v: E.wait_ge(s, v))

    def _deps(self, eng, reads, writes):
        deps = []
        own = self.cur[eng][0] if eng in self.cur else None
        for k in list(reads) + list(writes):
            x = self.lw.get(k)
            if x is not None:
                deps.append(x)
        for k in writes:
            for s, v in self.rd.get(k, {}).values():
                deps.append((s, v))
        if eng == "tensor":
            deps = [d for d in deps if d[0] is not own]
        return deps

    def _commit(self, tok, reads, writes):
        s, v = tok
        for k in writes:
            self.lw[k] = tok
            self.rd[k] = {}
        for k in reads:
            if k in writes:
                continue
            d = self.rd.setdefault(k, {})
            if d.get(id(s), (None, 0))[1] < v:
                d[id(s)] = (s, v)

    def op(self, eng, fn, reads=(), writes=()):
        deps = self._deps(eng, reads, writes)
        self._need(eng, deps)
        c = self.cur[eng]
        if c[1] >= SEM_CAP:
            self._new_phase(eng)
            c = self.cur[eng]
        c[1] += 1
        s, v = c[0], c[1]
        self.q[eng].append(lambda E, s=s: fn(E).then_inc(s, 1))
        self.n_ins[eng] += 1
        self._commit((s, v), reads, writes)

    def dma(self, eng, out, in_, reads=(), writes=(), **kw):
        i = self.dma_rr
        self.dma_rr = (self.dma_rr + 1) % N_DMA_SEMS
        s = self.dma_sems[i]
        deps = self._deps(eng, reads, writes)
        if self.dma_state[i] > 0:
            deps.append((s, self.dma_state[i]))
        self._need(eng, deps)
        self.dma_state[i] += 16
        v = self.dma_state[i]
        self.q[eng].append(lambda E, s=s: E.dma_start(out=out, in_=in_, **kw).then_inc(s, 16))
        self.n_ins[eng] += 1
        self._commit((s, v), reads, writes)

    def all_tokens(self):
        deps = [(s, v) for s, v in zip(self.dma_sems, self.dma_state) if v > 0]
        for e in self.cur:
            if self.cur[e][1] > 0:
                deps.append((self.cur[e][0], self.cur[e][1]))
        return deps

    def barrier(self):
        deps = self.all_tokens()
        for e in ENGS:
            self._need(e, deps)

    def finish(self):
        self._need("sync", self.all_tokens())

    def emit(self):
        with self.nc.Block() as block:
            for e in ENGS:
                lst = self.q[e]
                if not lst:
                    continue

                def body(E, lst=lst):
                    for f in lst:
                        f(E)
                getattr(block, e)(body)


class Arena:
    def __init__(self, t, nwords):
        self.t = t
        self.n = nwords
        self.off = 0

    def mark(self):
        return self.off

    def release(self, m):
        self.off = m

    def f32(self, n):
        o = self.off
        self.off += n
        assert self.off <= self.n, ("SBUF arena overflow", self.off, self.n)
        return self.t[:, o:o + n]

    def bf16(self, n):
        w = (n + 1) // 2
        return self.f32(w).bitcast(BF16)[:, 0:n]


class K:
    pass


def load_row_fm(k, dst, src_vec, eng="sync", key=None):
    k.p.dma(eng, dst, src_vec.rearrange("(c p) -> p c", p=128), writes=[key])


def stage_prenorm(k, l, xsrc):
    p, a = k.p, k.arena
    m = a.mark()
    g = a.f32(16)
    load_row_fm(k, g, k.din["pre_norm_g"][l], key="pn_g")
    xb = [a.f32(D) for _ in range(2)]
    xn = a.bf16(4 * D).rearrange("p (i d) -> p i d", i=4)
    junk = a.bf16(D)
    ss = a.f32(16)
    rstd = a.f32(16)
    p.op("vector", lambda E: E.memset(ss, 0.0), writes=["pn_ss"])
    ev = 0
    for tg in range(4):
        for ti in range(4):
            tt = tg * 4 + ti
            xt = xb[tt % 2]
            xk = ("pn_x", tt % 2)
            p.dma("sync", xt, xsrc[tt * 128:(tt + 1) * 128, :], writes=[xk])
            p.op("scalar", lambda E, xt=xt, tt=tt: E.activation(out=junk, in_=xt, func=AF.Square,
                                                                  accum_out=ss[:, tt:tt + 1]),
                 reads=[xk, "pn_ss"], writes=["pn_junk", ("pn_ss", tt)])
            p.op("scalar", lambda E, tt=tt: E.activation(out=rstd[:, tt:tt + 1], in_=ss[:, tt:tt + 1], func=AF.Sqrt,
                                                          bias=k.eps_t[:, 0:1], scale=1.0 / D),
                 reads=[("pn_ss", tt), "pn_ss", "eps"], writes=[("pn_r", tt)])
            p.op("vector", lambda E, tt=tt: E.reciprocal(out=rstd[:, tt:tt + 1], in_=rstd[:, tt:tt + 1]),
                 reads=[("pn_r", tt)], writes=[("pn_r", tt)])
            p.op("scalar", lambda E, xt=xt, tt=tt, ti=ti: E.activation(out=xn[:, ti, :], in_=xt, func=AF.Copy,
                                                                         scale=rstd[:, tt:tt + 1]),
                 reads=[xk, ("pn_r", tt)], writes=[("pn_xn", ti)])
        for kc in range(16):
            b = k.next_bank()
            psb = k.ps[b].bitcast(BF16)
            for ti in range(4):
                p.op("tensor", lambda E, ti=ti, kc=kc, psb=psb: E.transpose(
                    out=psb[:, ti * 128:(ti + 1) * 128], in_=xn[:, ti, kc * 128:(kc + 1) * 128], identity=k.ident_bf),
                    reads=[("pn_xn", ti), "ident"], writes=[("ps", b)])
            dst = k.uT[:, kc, tg * 512:(tg + 1) * 512]
            if ev % 2 == 0:
                p.op("vector", lambda E, dst=dst, psb=psb, kc=kc: E.tensor_scalar(
                    out=dst, in0=psb[:, 0:512], scalar1=g[:, kc:kc + 1], scalar2=None, op0=ALU.mult),
                    reads=[("ps", b), "pn_g"], writes=[("uT", kc, tg)])
            else:
                p.op("scalar", lambda E, dst=dst, psb=psb, kc=kc: E.activation(
                    out=dst, in_=psb[:, 0:512], func=AF.Copy, scale=g[:, kc:kc + 1]),
                    reads=[("ps", b), "pn_g"], writes=[("uT", kc, tg)])
            ev += 1
    p.barrier()
    a.release(m)


def bc_last(ap2, n):
    return ap2.unsqueeze(2).broadcast_to([ap2.shape[0], ap2.shape[1], n])


def bc_mid(ap2, n):
    return ap2.unsqueeze(1).broadcast_to([ap2.shape[0], n, ap2.shape[1]])


def load_w(k, wt, l, segs, key):
    off = 0
    keys = []
    for i, (c0, n) in enumerate(segs):
        kk = (key, i)
        k.p.dma("gpsimd", wt[:, :, off:off + n],
                k.din["w_in"][l][:, c0:c0 + n].rearrange("(c p) n -> p c n", p=128), writes=[kk])
        keys.append(kk)
        off += n
    return keys


def evac_copy(k, i, out, in_, reads, writes):
    if i % 2 == 0:
        k.p.op("vector", lambda E: E.tensor_copy(out=out, in_=in_), reads=reads, writes=writes)
    else:
        k.p.op("scalar", lambda E: E.activation(out=out, in_=in_, func=AF.Copy), reads=reads, writes=writes)


def epilogue(k, o3, gain, gate3, eps_ap, inv_n, mrow0, tag):
    p, a = k.p, k.arena
    m = a.mark()
    sq = a.f32(2048).rearrange("p (a b) -> p a b", a=16)
    ssq = a.f32(16)
    ob = a.bf16(2048).rearrange("p (a b) -> p a b", a=16)
    mixT = a.bf16(2048)
    ko = (tag, "o")
    p.op("gpsimd", lambda E: E.tensor_tensor(out=sq, in0=o3, in1=o3, op=ALU.mult), reads=[ko], writes=[(tag, "sq")])
    p.op("vector", lambda E: E.tensor_reduce(out=ssq, in_=sq, axis=AX.X, op=ALU.add), reads=[(tag, "sq")],
         writes=[(tag, "ssq")])
    p.op("scalar", lambda E: E.activation(out=ssq, in_=ssq, func=AF.Sqrt, bias=eps_ap, scale=inv_n),
         reads=[(tag, "ssq"), "eps"], writes=[(tag, "ssq")])
    p.op("vector", lambda E: E.reciprocal(out=ssq, in_=ssq), reads=[(tag, "ssq")], writes=[(tag, "ssq")])
    p.op("vector", lambda E: E.tensor_tensor(out=o3, in0=o3, in1=bc_last(ssq, 128), op=ALU.mult),
         reads=[ko, (tag, "ssq")], writes=[ko])
    p.op("gpsimd", lambda E: E.tensor_tensor(out=o3, in0=o3, in1=bc_mid(gain, 16), op=ALU.mult),
         reads=[ko, (tag, "gain")], writes=[ko])
    p.op("vector", lambda E: E.tensor_tensor(out=ob, in0=o3, in1=gate3, op=ALU.mult),
         reads=[ko, (tag, "gate")], writes=[(tag, "ob")])
    for half in range(2):
        b = k.next_bank()
        psb = k.ps[b].bitcast(BF16)
        for i in range(8):
            tt = half * 8 + i
            p.op("tensor", lambda E, i=i, tt=tt, psb=psb: E.transpose(out=psb[:, i * 128:(i + 1) * 128],
                                                                     in_=ob[:, tt, :], identity=k.ident_bf),
                 reads=[(tag, "ob"), "ident"], writes=[("ps", b)])
        evac_copy(k, half, mixT[:, half * 1024:(half + 1) * 1024], psb[:, 0:1024], [("ps", b)], [(tag, "mixT", half)])
    p.dma("sync", k.mixedT_d[mrow0:mrow0 + 128, :], mixT, reads=[(tag, "mixT", 0), (tag, "mixT", 1)],
          writes=[("mixedT", mrow0)])
    a.release(m)


def setup_bias_table(k):
    p, a = k.p, k.arena
    m = a.mark()
    rb = a.f32(8)
    oh = a.f32(1280)
    fsb = a.f32(1280)
    p.dma("sync", rb[0:32, :], k.din["rel_bias"], writes=["rb"])
    p.dma("sync", oh[0:32, :], k.din["c_oh"], writes=["oh"])
    for c in range(3):
        n = 512 if c < 2 else 256
        b = k.next_bank()
        p.op("tensor", lambda E, b=b, c=c, n=n: E.matmul(k.ps[b][0:8, 0:n], lhsT=rb[0:32, :],
                                                         rhs=oh[0:32, c * 512:c * 512 + n], start=True, stop=True),
             reads=["rb", "oh"], writes=[("ps", b)])
        p.op("vector", lambda E, b=b, c=c, n=n: E.tensor_copy(out=fsb[0:8, c * 512:c * 512 + n], in_=k.ps[b][0:8, 0:n]),
             reads=[("ps", b)], writes=[("fsb", c)])
    p.dma("sync", k.F_h.ap(), fsb[0:8, :], reads=[("fsb", c) for c in range(3)], writes=["F_d"])
    p.barrier()
    a.release(m)


def stage_attn(k, l, heads):
    p, a = k.p, k.arena
    m0 = a.mark()
    heads = list(heads)
    lam_init = 0.8 - 0.6 * math.exp(-0.3 * l)
    lq = a.f32(256).rearrange("p (a b) -> p a b", a=4)
    for i, n in enumerate(["lambda_q1", "lambda_k1", "lambda_q2", "lambda_k2"]):
        p.dma("sync", lq[:, i, :], k.din[n][l].partition_broadcast(128), writes=[("lam_in", i)])
    lprod = a.f32(128).rearrange("p (a b) -> p a b", a=2)
    lsum = a.f32(2)
    neglam = a.f32(1)
    for i in range(2):
        p.op("vector", lambda E, i=i: E.tensor_tensor(out=lprod[:, i, :], in0=lq[:, 2 * i, :], in1=lq[:, 2 * i + 1, :],
                                                      op=ALU.mult),
             reads=[("lam_in", 2 * i), ("lam_in", 2 * i + 1)], writes=[("lprod", i)])
    p.op("vector", lambda E: E.tensor_reduce(out=lsum, in_=lprod, axis=AX.X, op=ALU.add),
         reads=[("lprod", 0), ("lprod", 1)], writes=["lsum"])
    p.op("scalar", lambda E: E.activation(out=lsum, in_=lsum, func=AF.Exp), reads=["lsum"], writes=["lsum"])
    p.op("vector", lambda E: E.scalar_tensor_tensor(out=neglam, in0=lsum[:, 1:2], scalar=-lam_init, in1=lsum[:, 0:1],
                                                    op0=ALU.add, op1=ALU.subtract), reads=["lsum"], writes=["neglam"])
    gsub = a.f32(128)
    p.dma("sync", gsub, k.din["subln_g"][l].partition_broadcast(128), writes=[("attn", "gain")])
    p.op("vector", lambda E: E.tensor_scalar(out=gsub, in0=gsub, scalar1=1.0 - lam_init, scalar2=None, op0=ALU.mult),
         reads=[("attn", "gain")], writes=[("attn", "gain")])
    bfar = a.f32(16).rearrange("p (a b) -> p a b", a=2)
    p.dma("sync", bfar, bass.AP(k.dh["rel_bias"], 15 * 8, [[0, 128], [128, 2], [1, 8]]), writes=["bfar"])

    wts = [a.bf16(16 * 512).rearrange("p (c n) -> p c n", c=16) for _ in range(2)]
    qT = a.bf16(2048)
    kT = a.bf16(2048)
    Vaug = a.bf16(16 * 130).rearrange("p (t c) -> p t c", t=16)
    gA = a.f32(2048).rearrange("p (a b) -> p a b", a=16)
    biasT = [a.f32(256) for _ in range(4)]
    tprime = [a.f32(256) for _ in range(2)]
    PT = [a.bf16(16 * 256).rearrange("p (kb q) -> p kb q", kb=16) for _ in range(2)]
    o1 = a.f32(2048).rearrange("p (a b) -> p a b", a=16)
    oA = a.f32(2048).rearrange("p (a b) -> p a b", a=16)
    tmpS = [a.f32(256) for _ in range(2)]
    rs = a.f32(32)
    p.op("vector", lambda E: E.memset(Vaug[:, :, 128:130], 1.0), writes=["Vones"])

    def wsegs(h):
        return [(h * 128, 128), (1024 + h * 128, 128), (2048 + h * 128, 128), (3072 + h * 128, 128)]

    wkeys_next = load_w(k, wts[0], l, wsegs(heads[0]), ("wA", 0))
    for hi, h in enumerate(heads):
        wt = wts[hi % 2]
        wkeys = wkeys_next
        if hi + 1 < len(heads):
            wkeys_next = load_w(k, wts[(hi + 1) % 2], l, wsegs(heads[hi + 1]), ("wA", (hi + 1) % 2))
        for Di, Dv in enumerate([-1, 0, 1, 2]):
            c = 512 - 128 * Dv
            tp = tprime[Di % 2]
            p.dma("sync", tp, bass.AP(k.F_h, h * 1280 + c, [[1, 128], [1, 256]]), reads=["F_d"],
                  writes=[("tprime", Di % 2)])
            b = k.next_bank()
            p.op("tensor", lambda E, b=b, tp=tp: E.matmul(k.ps[b][:, 0:256], lhsT=k.J, rhs=tp, start=True, stop=True),
                 reads=["J", ("tprime", Di % 2)], writes=[("ps", b)])
            evac_copy(k, Di, biasT[Di], k.ps[b][:, 0:256], [("ps", b)], [("biasT", Di)])
        ev = 0
        for which, dst in ((0, qT), (1, kT)):
            for tq in range(4):
                b = k.next_bank()
                for kc in range(16):
                    p.op("tensor", lambda E, b=b, kc=kc, which=which, tq=tq, wt=wt: E.matmul(
                        k.ps[b], lhsT=wt[:, kc, which * 128:(which + 1) * 128],
                        rhs=k.uT[:, kc, tq * 512:(tq + 1) * 512], start=(kc == 0), stop=(kc == 15)),
                        reads=wkeys + [("uT", kc, tq)], writes=[("ps", b)])
                evac_copy(k, ev, dst[:, tq * 512:(tq + 1) * 512], k.ps[b], [("ps", b)], [("qk", which, tq)])
                ev += 1
        for tt in range(16):
            b = k.next_bank()
            for kc in range(16):
                p.op("tensor", lambda E, b=b, kc=kc, tt=tt, wt=wt: E.matmul(
                    k.ps[b][:, 0:256], lhsT=k.uT[:, kc, tt * 128:(tt + 1) * 128], rhs=wt[:, kc, 256:512],
                    start=(kc == 0), stop=(kc == 15)),
                    reads=wkeys + [("uT", kc, tt // 4)], writes=[("ps", b)])
            p.op("vector", lambda E, b=b, tt=tt: E.tensor_copy(out=Vaug[:, tt, 0:128], in_=k.ps[b][:, 0:128]),
                 reads=[("ps", b)], writes=[("V", tt), ("psr", b)])
            p.op("scalar", lambda E, b=b, tt=tt: E.activation(out=gA[:, tt, :], in_=k.ps[b][:, 128:256], func=AF.Silu),
                 reads=[("ps", b)], writes=[("attn", "gate"), ("psr", b)])
        for mm in range(2):
            for qt in range(8):
                pi = (mm * 8 + qt) % 2
                PTb = PT[pi]
                for kb in range(16):
                    b = k.next_bank()
                    p.op("tensor", lambda E, b=b, kb=kb, qt=qt, mm=mm: E.matmul(
                        k.ps[b][:, 0:256], lhsT=kT[mm * 64:(mm + 1) * 64, kb * 128:(kb + 1) * 128],
                        rhs=qT[mm * 64:(mm + 1) * 64, qt * 256:(qt + 1) * 256], start=True, stop=True),
                        reads=[("qk", 1, kb // 4), ("qk", 0, qt // 2)], writes=[("ps", b)])
                    Dv = kb - 2 * qt
                    if -1 <= Dv <= 2:
                        ts_ = tmpS[kb % 2]
                        p.op("vector", lambda E, b=b, ts_=ts_, Dv=Dv: E.scalar_tensor_tensor(
                            out=ts_, in0=k.ps[b][:, 0:256], scalar=0.125, in1=biasT[Dv + 1], op0=ALU.mult, op1=ALU.add),
                            reads=[("ps", b), ("biasT", Dv + 1)], writes=[("tmpS", kb % 2)])
                        p.op("scalar", lambda E, ts_=ts_, kb=kb, PTb=PTb: E.activation(out=PTb[:, kb, :], in_=ts_,
                                                                                      func=AF.Exp),
                             reads=[("tmpS", kb % 2)], writes=[("PT", pi, kb)])
                    else:
                        side = 1 if Dv >= 3 else 0
                        p.op("scalar", lambda E, b=b, kb=kb, PTb=PTb, side=side, h=h: E.activation(
                            out=PTb[:, kb, :], in_=k.ps[b][:, 0:256], func=AF.Exp, bias=bfar[:, side, h:h + 1],
                            scale=0.125), reads=[("ps", b), "bfar"], writes=[("PT", pi, kb)])
                for qs in range(2):
                    qb = qt * 2 + qs
                    b = k.next_bank()
                    for kb in range(16):
                        p.op("tensor", lambda E, b=b, kb=kb, qs=qs, PTb=PTb: E.matmul(
                            k.ps[b][:, 0:130], lhsT=PTb[:, kb, qs * 128:(qs + 1) * 128], rhs=Vaug[:, kb, 0:130],
                            start=(kb == 0), stop=(kb == 15)),
                            reads=[("PT", pi, kb), ("V", kb), "Vones"], writes=[("ps", b)])
                    rsq = rs[:, mm * 16 + qb:mm * 16 + qb + 1]
                    rk = ("rs", mm, qb)
                    p.op("vector", lambda E, b=b, rsq=rsq: E.reciprocal(out=rsq, in_=k.ps[b][:, 128:129]),
                         reads=[("ps", b)], writes=[rk])
                    if mm == 0:
                        p.op("vector", lambda E, b=b, rsq=rsq, qb=qb: E.tensor_scalar(
                            out=o1[:, qb, :], in0=k.ps[b][:, 0:128], scalar1=rsq, scalar2=None, op0=ALU.mult),
                            reads=[("ps", b), rk], writes=[("o1", qb)])
                    else:
                        p.op("vector", lambda E, rsq=rsq: E.tensor_scalar(out=rsq, in0=rsq, scalar1=neglam[:, 0:1],
                                                                          scalar2=None, op0=ALU.mult),
                             reads=[rk, "neglam"], writes=[rk])
                        p.op("vector", lambda E, b=b, rsq=rsq, qb=qb: E.scalar_tensor_tensor(
                            out=oA[:, qb, :], in0=k.ps[b][:, 0:128], scalar=rsq, in1=o1[:, qb, :], op0=ALU.mult,
                            op1=ALU.add), reads=[("ps", b), rk, ("o1", qb)], writes=[("attn", "o")])
        epilogue(k, oA, gsub, gA, k.eps_t[:, 1:2], 1.0 / 128, h * 128, "attn")
    p.barrier()
    a.release(m0)


def stage_out(k, l, xsrc, ydst):
    p, a = k.p, k.arena
    m = a.mark()
    wo = a.bf16(24 * D).rearrange("p (c d) -> p c d", c=24)
    wkeys = []
    for c4 in range(6):
        p.dma("gpsimd", wo[:, c4 * 4:(c4 + 1) * 4, :],
              k.din["w_out"][l][c4 * 512:(c4 + 1) * 512, :].rearrange("(c p) d -> p c d", p=128), writes=[("wo", c4)])
        wkeys.append(("wo", c4))
    a2 = Arena(k.uT_f32, 16 * S // 2)
    gpost = a2.f32(D)
    p.dma("sync", gpost, k.din["post_norm_g"][l].partition_broadcast(128), writes=["gpost"])
    mts = [a2.bf16(24 * 128).rearrange("p (c t) -> p c t", c=24) for _ in range(2)]
    xts = [a2.f32(D) for _ in range(2)]
    ots = [a2.f32(D) for _ in range(2)]
    junk = a2.bf16(D)
    ss = a.f32(16)
    p.op("vector", lambda E: E.memset(ss, 0.0), writes=["o_ss"])
    mrows = [i * 128 for i in range(24)]
    for tt in range(16):
        mt, xt, ot = mts[tt % 2], xts[tt % 2], ots[tt % 2]
        kmt, kxt, kot = ("o_mt", tt % 2), ("o_xt", tt % 2), ("o_ot", tt % 2)
        p.dma("sync", mt, k.mixedT_d[:, tt * 128:(tt + 1) * 128].rearrange("(c p) t -> p c t", p=128),
              reads=[("mixedT", r) for r in mrows], writes=[kmt])
        p.dma("sync", xt, xsrc[tt * 128:(tt + 1) * 128, :], writes=[kxt])
        for n in range(4):
            b = k.next_bank()
            for c in range(24):
                p.op("tensor", lambda E, b=b, c=c, n=n, mt=mt: E.matmul(
                    k.ps[b], lhsT=mt[:, c, :], rhs=wo[:, c, n * 512:(n + 1) * 512], start=(c == 0), stop=(c == 23)),
                    reads=[kmt] + wkeys, writes=[("ps", b)])
            evac_copy(k, n, ot[:, n * 512:(n + 1) * 512], k.ps[b], [("ps", b)], [(kot, n)])
        okeys = [(kot, n) for n in range(4)]
        p.op("scalar", lambda E, ot=ot, tt=tt: E.activation(out=junk, in_=ot, func=AF.Square,
                                                            accum_out=ss[:, tt:tt + 1]),
             reads=okeys + ["o_ss"], writes=["o_junk", ("o_ssv", tt)])
        p.op("scalar", lambda E, tt=tt: E.activation(out=ss[:, tt:tt + 1], in_=ss[:, tt:tt + 1], func=AF.Sqrt,
                                                     bias=k.eps_t[:, 0:1], scale=1.0 / D),
             reads=[("o_ssv", tt), "eps"], writes=[("o_ssv", tt)])
        p.op("vector", lambda E, tt=tt: E.reciprocal(out=ss[:, tt:tt + 1], in_=ss[:, tt:tt + 1]),
             reads=[("o_ssv", tt)], writes=[("o_ssv", tt)])
        p.op("vector", lambda E, ot=ot, tt=tt: E.scalar_tensor_tensor(out=ot, in0=ot, scalar=ss[:, tt:tt + 1], in1=gpost,
                                                                     op0=ALU.mult, op1=ALU.mult),
             reads=okeys + [("o_ssv", tt), "gpost"], writes=okeys)
        p.op("gpsimd", lambda E, ot=ot, xt=xt: E.tensor_tensor(out=ot, in0=ot, in1=xt, op=ALU.add),
             reads=okeys + [kxt], writes=okeys)
        p.dma("sync", ydst[tt * 128:(tt + 1) * 128, :], ot, reads=okeys, writes=[("xout", l, tt)])
    p.barrier()
    a.release(m)


def stage_hgrn(k, l, heads):
    p, a = k.p, k.arena
    m0 = a.mark()
    heads = list(heads)
    lbt = a.f32(16).rearrange("p (d h) -> p d h", d=2)
    oml = a.f32(16).rearrange("p (d h) -> p d h", d=2)
    if l == 0:
        p.op("vector", lambda E: E.memset(lbt, 0.0), writes=["lbt"])
        p.op("vector", lambda E: E.memset(oml, 1.0), writes=["oml"])
    else:
        lg = a.f32(32).rearrange("p (d l h) -> p d l h", d=2, l=2)
        s0 = a.f32(16).rearrange("p (d h) -> p d h", d=2)
        s1 = a.f32(16).rearrange("p (d h) -> p d h", d=2)
        lk = []
        for d in range(2):
            for ll in range(2):
                p.dma("sync", lg[:, d, ll, :], k.din["hgrn_lb_logits"][d, ll].rearrange("(h p) -> p h", p=128),
                      writes=[("lg", d, ll)])
                lk.append(("lg", d, ll))
        p.op("vector", lambda E: E.tensor_tensor(out=s0, in0=lg[:, :, 0, :], in1=lg[:, :, 1, :], op=ALU.subtract),
             reads=lk, writes=["lb_s0"])
        p.op("scalar", lambda E: E.activation(out=s1, in_=s0, func=AF.Sigmoid, scale=-1.0), reads=["lb_s0"],
             writes=["lb_s1"])
        p.op("scalar", lambda E: E.activation(out=s0, in_=s0, func=AF.Sigmoid), reads=["lb_s0", "lb_s1"],
             writes=["lb_s0"])
        p.op("vector", lambda E: E.tensor_tensor(out=s1, in0=s0, in1=s1, op=ALU.add), reads=["lb_s0", "lb_s1"],
             writes=["lb_s1"])
        p.op("vector", lambda E: E.tensor_tensor(out=lbt, in0=s1, in1=s0, op=ALU.subtract), reads=["lb_s0", "lb_s1"],
             writes=["lbt"])
        p.op("vector", lambda E: E.tensor_scalar(out=oml, in0=lbt, scalar1=-1.0, scalar2=1.0, op0=ALU.mult,
                                                 op1=ALU.add), reads=["lbt"], writes=["oml"])
    cm0 = a.f32(S)
    p.op("vector", lambda E: E.memset(cm0, 1.0), writes=["cm0"])
    p.op("vector", lambda E: E.memset(cm0.rearrange("p (c t) -> p c t", t=64)[:, :, 0:1], 0.0), writes=["cm0"])
    gainC = a.f32(128)
    p.dma("sync", gainC, k.din["hgrn_norm_g"][l].partition_broadcast(128), writes=[("hgrn", "gain")])
    masks = [a.f32(128), a.f32(128)]
    p.dma("sync", masks[0], k.din["c_maskF"], writes=["maskF"])
    p.dma("sync", masks[1], k.din["c_maskB"], writes=["maskB"])
    wt = a.bf16(16 * 640).rearrange("p (c n) -> p c n", c=16)
    qT = a.f32(S)
    B1, B2, B3, B4, B5 = [a.f32(S) for _ in range(5)]
    qtl, ktl, qha, qhb = [a.bf16(S) for _ in range(4)]
    khat = a.bf16(S).rearrange("p (t j) -> p t j", t=16)
    V = a.bf16(S).rearrange("p (t j) -> p t j", t=16)
    gC = a.f32(S).rearrange("p (t j) -> p t j", t=16)
    oacc = a.f32(S).rearrange("p (t j) -> p t j", t=16)
    Sf = a.f32(128)
    Sbs = [a.bf16(128), a.bf16(128)]
    scT = a.bf16(128)
    totb = a.f32(32)
    refb = a.f32(32)
    dec = a.f32(32)
    p.op("vector", lambda E: E.memset(qha, 0.0), writes=["qha"])
    p.op("vector", lambda E: E.memset(qhb, 0.0), writes=["qhb"])

    def v3(x):
        return x.rearrange("p (c t) -> p c t", t=64)

    def v4(x):
        return x.rearrange("p (n h t) -> p n h t", h=2, t=64)

    for hi, h in enumerate(heads):
        segs = [(8448 + h * 128, 128), (10496 + h * 128, 128), (11520 + h * 128, 128), (9472 + h * 128, 128),
                (12544 + h * 128, 128)]
        wkeys = load_w(k, wt, l, segs, "wC")
        for tq in range(4):
            b = k.next_bank()
            for kc in range(16):
                p.op("tensor", lambda E, b=b, kc=kc, tq=tq: E.matmul(
                    k.ps[b], lhsT=wt[:, kc, 0:128], rhs=k.uT[:, kc, tq * 512:(tq + 1) * 512], start=(kc == 0),
                    stop=(kc == 15)), reads=wkeys + [("uT", kc, tq)], writes=[("ps", b)])
            evac_copy(k, tq, qT[:, tq * 512:(tq + 1) * 512], k.ps[b], [("ps", b)], ["c_qT"])
        for tt in range(16):
            b = k.next_bank()
            for kc in range(16):
                p.op("tensor", lambda E, b=b, kc=kc, tt=tt: E.matmul(
                    k.ps[b][:, 0:256], lhsT=k.uT[:, kc, tt * 128:(tt + 1) * 128], rhs=wt[:, kc, 384:640],
                    start=(kc == 0), stop=(kc == 15)), reads=wkeys + [("uT", kc, tt // 4)], writes=[("ps", b)])
            p.op("vector", lambda E, b=b, tt=tt: E.tensor_copy(out=V[:, tt, :], in_=k.ps[b][:, 0:128]),
                 reads=[("ps", b)], writes=[("cV", tt), ("psr", b)])
            p.op("scalar", lambda E, b=b, tt=tt: E.activation(out=gC[:, tt, :], in_=k.ps[b][:, 128:256], func=AF.Silu),
                 reads=[("ps", b)], writes=[("hgrn", "gate"), ("psr", b)])
        for d in range(2):
            for tq in range(4):
                b = k.next_bank()
                for kc in range(16):
                    p.op("tensor", lambda E, b=b, kc=kc, tq=tq, d=d: E.matmul(
                        k.ps[b], lhsT=wt[:, kc, 128 + 128 * d:256 + 128 * d], rhs=k.uT[:, kc, tq * 512:(tq + 1) * 512],
                        start=(kc == 0), stop=(kc == 15)), reads=wkeys + [("uT", kc, tq)], writes=[("ps", b)])
                p.op("scalar", lambda E, b=b, tq=tq: E.activation(out=B1[:, tq * 512:(tq + 1) * 512], in_=k.ps[b],
                                                                  func=AF.Sigmoid), reads=[("ps", b)], writes=["B1"])
            p.op("vector", lambda E, d=d, h=h: E.tensor_scalar(out=B1, in0=B1, scalar1=oml[:, d, h:h + 1],
                                                               scalar2=lbt[:, d, h:h + 1], op0=ALU.mult, op1=ALU.add),
                 reads=["B1", "oml", "lbt"], writes=["B1"])
            p.op("scalar", lambda E: E.activation(out=B2, in_=B1, func=AF.Ln), reads=["B1"], writes=["B2"])
            p.op("vector", lambda E: E.tensor_scalar(out=B1, in0=B1, scalar1=-1.0, scalar2=1.0, op0=ALU.mult,
                                                     op1=ALU.add), reads=["B1", "B2"], writes=["B1"])
            p.op("vector", lambda E: E.tensor_tensor_scan(out=B3, data0=cm0, data1=B2, initial=0.0, op0=ALU.mult,
                                                          op1=ALU.add), reads=["cm0", "B2"], writes=["B3"])
            p.op("vector", lambda E: E.tensor_copy(out=totb, in_=v3(B3)[:, :, 63]), reads=["B3"], writes=["totb"])
            if d == 1:
                p.op("vector", lambda E: E.scalar_tensor_tensor(out=B3, in0=B3, scalar=-1.0, in1=B2, op0=ALU.mult,
                                                                op1=ALU.add), reads=["B3", "B2", "totb"], writes=["B3"])
                p.op("vector", lambda E: E.tensor_tensor(out=v3(B3), in0=v3(B3), in1=bc_last(totb, 64), op=ALU.add),
                     reads=["B3", "totb"], writes=["B3"])
            mid = 31 if d == 0 else 32
            p.op("vector", lambda E, mid=mid: E.tensor_copy(out=refb, in_=v3(B3)[:, :, mid]), reads=["B3"],
                 writes=["refb"])
            p.op("scalar", lambda E: E.activation(out=dec, in_=totb, func=AF.Exp), reads=["totb"], writes=["dec"])
            p.op("vector", lambda E: E.tensor_tensor(out=v3(B4), in0=v3(B3), in1=bc_last(refb, 64), op=ALU.subtract),
                 reads=["B3", "refb"], writes=["B4"])
            p.op("scalar", lambda E: E.activation(out=B2, in_=B4, func=AF.Exp), reads=["B4"], writes=["B2"])
            p.op("vector", lambda E: E.tensor_tensor(out=qtl, in0=qT, in1=B2, op=ALU.mult), reads=["c_qT", "B2"],
                 writes=["qtl"])
            p.op("scalar", lambda E: E.activation(out=B2, in_=B4, func=AF.Exp, scale=-1.0), reads=["B4", "qtl"],
                 writes=["B2"])
            p.op("vector", lambda E: E.tensor_tensor(out=ktl, in0=B1, in1=B2, op=ALU.mult), reads=["B1", "B2"],
                 writes=["ktl"])
            p.op("scalar", lambda E: E.activation(out=B2, in_=B3, func=AF.Exp), reads=["B3", "ktl"], writes=["B2"])
            p.op("vector", lambda E: E.tensor_tensor(out=v4(qha)[:, :, 0, :], in0=v4(qT)[:, :, 0, :],
                                                     in1=v4(B2)[:, :, 0, :], op=ALU.mult), reads=["c_qT", "B2"],
                 writes=["qha"])
            p.op("vector", lambda E: E.tensor_tensor(out=v4(qhb)[:, :, 1, :], in0=v4(qT)[:, :, 1, :],
                                                     in1=v4(B2)[:, :, 1, :], op=ALU.mult), reads=["c_qT", "B2"],
                 writes=["qhb"])
            p.op("vector", lambda E: E.scalar_tensor_tensor(out=v3(B4), in0=v3(B3), scalar=-1.0, in1=bc_last(totb, 64),
                                                            op0=ALU.mult, op1=ALU.add), reads=["B3", "totb", "B2"],
                 writes=["B4"])
            p.op("scalar", lambda E: E.activation(out=B2, in_=B4, func=AF.Exp), reads=["B4", "qha", "qhb"],
                 writes=["B2"])
            p.op("vector", lambda E: E.tensor_tensor(out=B5, in0=B1, in1=B2, op=ALU.mult), reads=["B1", "B2"],
                 writes=["B5"])
            for g4 in range(4):
                b = k.next_bank()
                for i in range(4):
                    tt = g4 * 4 + i
                    p.op("tensor", lambda E, b=b, i=i, tt=tt: E.transpose(
                        out=k.ps[b][:, i * 128:(i + 1) * 128], in_=B5[:, tt * 128:(tt + 1) * 128], identity=k.ident_f),
                        reads=["B5", "ident_f"], writes=[("ps", b)])
                evac_copy(k, g4, khat[:, g4 * 4:(g4 + 1) * 4, :],
                          k.ps[b].rearrange("p (t j) -> p t j", t=4), [("ps", b)], [("khat", g4)])
            p.op("vector", lambda E: E.memset(Sf, 0.0), writes=["Sf"])
            p.op("vector", lambda E: E.memset(Sbs[0], 0.0), writes=[("Sb", 0)])
            si = 0
            tts = range(16) if d == 0 else range(15, -1, -1)
            halves = (0, 1) if d == 0 else (1, 0)
            qhs = (qha, qhb)
            for tt in tts:
                cs = slice(tt * 128, (tt + 1) * 128)
                bS = k.next_bank()
                p.op("tensor", lambda E, bS=bS, cs=cs: E.matmul(k.ps[bS][:, 0:128], lhsT=ktl[:, cs], rhs=qtl[:, cs],
                                                                start=True, stop=True),
                     reads=["ktl", "qtl"], writes=[("ps", bS)])
                p.op("vector", lambda E, bS=bS, d=d: E.tensor_tensor(out=scT, in0=k.ps[bS][:, 0:128], in1=masks[d],
                                                                     op=ALU.mult),
                     reads=[("ps", bS), "maskF", "maskB"], writes=["scT"])
                bO = k.next_bank()
                p.op("tensor", lambda E, bO=bO, tt=tt: E.matmul(k.ps[bO][:, 0:128], lhsT=scT, rhs=V[:, tt, :],
                                                                start=True, stop=False),
                     reads=["scT", ("cV", tt)], writes=[("ps", bO)])
                for hn, hf in enumerate(halves):
                    p.op("tensor", lambda E, bO=bO, cs=cs, hf=hf, si=si, hn=hn: E.matmul(
                        k.ps[bO][:, 0:128], lhsT=qhs[hf][:, cs], rhs=Sbs[si], start=False, stop=(hn == 1)),
                        reads=["qha", "qhb", ("Sb", si)], writes=[("ps", bO)])
                    bU = k.next_bank()
                    ps_ = slice(hf * 64, (hf + 1) * 64)
                    p.op("tensor", lambda E, bU=bU, ps_=ps_, tt=tt: E.matmul(
                        k.ps[bU][:, 0:128], lhsT=khat[ps_, tt, :], rhs=V[ps_, tt, :], start=True, stop=True),
                        reads=[("khat", tt // 4), ("cV", tt)], writes=[("ps", bU)])
                    ch = tt * 2 + hf
                    p.op("vector", lambda E, bU=bU, ch=ch: E.scalar_tensor_tensor(
                        out=Sf, in0=Sf, scalar=dec[:, ch:ch + 1], in1=k.ps[bU][:, 0:128], op0=ALU.mult, op1=ALU.add),
                        reads=["Sf", "dec", ("ps", bU)], writes=["Sf"])
                    si ^= 1
                    p.op("scalar", lambda E, si=si: E.activation(out=Sbs[si], in_=Sf, func=AF.Copy), reads=["Sf"],
                         writes=[("Sb", si)])
                if d == 0:
                    p.op("scalar", lambda E, bO=bO, tt=tt: E.activation(out=oacc[:, tt, :], in_=k.ps[bO][:, 0:128],
                                                                        func=AF.Copy),
                         reads=[("ps", bO)], writes=[("oacc", tt)])
                else:
                    p.op("vector", lambda E, bO=bO, tt=tt: E.tensor_tensor(out=oacc[:, tt, :], in0=oacc[:, tt, :],
                                                                           in1=k.ps[bO][:, 0:128], op=ALU.add),
                         reads=[("ps", bO), ("oacc", tt)], writes=[("oacc", tt), ("hgrn", "o")])
        epilogue(k, oacc, gainC, gC, k.eps_t[:, 0:1], 1.0 / 128, 2048 + h * 128, "hgrn")
    p.barrier()
    a.release(m0)


def pslice(buf, p0, n, step):
    ap = [list(x) for x in buf.ap]
    ps = ap[0][0]
    return bass.AP(buf.tensor, buf.offset + p0 * ps, [[ps * step, n]] + ap[1:])


def stage_rwkv(k, l):
    p, a = k.p, k.arena
    a2 = Arena(k.uT_f32, 16 * S // 2)
    rw = {n: t.ap() for n, t in k.rw.items()}
    TT = 16
    m0 = a.mark()
    wts = [a.bf16(16 * 512).rearrange("p (c n) -> p c n", c=16) for _ in range(2)]
    stg = [a.f32(512) for _ in range(3)]
    cts = [(4096 + ct * 512, 512 if ct < 8 else 256) for ct in range(9)]
    wk_next = load_w(k, wts[0], l, [cts[0]], ("wB", 0))
    si = 0
    for ct, (c0, n) in enumerate(cts):
        wt = wts[ct % 2]
        wkeys = wk_next
        if ct + 1 < 9:
            wk_next = load_w(k, wts[(ct + 1) % 2], l, [cts[ct + 1]], ("wB", (ct + 1) % 2))
        for tt in range(TT):
            b = k.next_bank()
            for kc in range(16):
                p.op("tensor", lambda E, b=b, kc=kc, tt=tt, wt=wt, n=n: E.matmul(
                    k.ps[b][:, 0:n], lhsT=k.uT[:, kc, tt * 128:(tt + 1) * 128], rhs=wt[:, kc, 0:n], start=(kc == 0),
                    stop=(kc == 15)), reads=wkeys + [("uT", kc, tt // 4)], writes=[("ps", b)])
            sb = stg[si % 3]
            evac_copy(k, si, sb[:, 0:n], k.ps[b][:, 0:n], [("ps", b)], [("stgB", si % 3)])
            p.dma("sync", k.pB_d[tt * 128:(tt + 1) * 128, ct * 512:ct * 512 + n], sb[:, 0:n], reads=[("stgB", si % 3)],
                  writes=[("pB", tt, ct)])
            si += 1
    p.barrier()
    a.release(m0)
    m0 = a.mark()
    mu0b, mu1b = a2.f32(3328), a2.f32(3328)
    w0b = a2.f32(2048).rearrange("p (d c) -> p d c", d=2)
    a0b = a2.f32(2048).rearrange("p (d c) -> p d c", d=2)
    k_kb, k_ab, r_kb = a2.f32(1024), a2.f32(1024), a2.f32(1024)
    wup, aup = a2.bf16(1024), a2.bf16(1024)
    prm = ["rwp%d" % i for i in range(9)]
    p.dma("sync", mu0b, k.din["rwkv_shift_mu"][l, 0].partition_broadcast(128), writes=[prm[0]])
    p.dma("sync", mu1b, k.din["rwkv_shift_mu"][l, 1].partition_broadcast(128), writes=[prm[1]])
    p.dma("sync", w0b.rearrange("p d c -> p (d c)"),
          k.din["rwkv_w0"][l].rearrange("d c -> (d c)").partition_broadcast(128), writes=[prm[2]])
    p.dma("sync", a0b.rearrange("p d c -> p (d c)"),
          k.din["rwkv_a0"][l].rearrange("d c -> (d c)").partition_broadcast(128), writes=[prm[3]])
    p.dma("sync", k_kb, k.din["rwkv_k_k"][l].partition_broadcast(128), writes=[prm[4]])
    p.dma("sync", k_ab, k.din["rwkv_k_a"][l].partition_broadcast(128), writes=[prm[5]])
    p.dma("sync", r_kb, k.din["rwkv_r_k"][l].partition_broadcast(128), writes=[prm[6]])
    p.dma("gpsimd", wup, k.din["rwkv_w_up"][l].rearrange("d l c -> (d l) c"), writes=[prm[7]])
    p.dma("gpsimd", aup, k.din["rwkv_a_up"][l].rearrange("d l c -> (d l) c"), writes=[prm[8]])
    P0, Pp, Pn = a.f32(3328), a.f32(3328), a.f32(3328)
    KKt, RK, tmp = a.f32(1024), a.f32(1024), a.f32(1024)
    Wt, At, KAt, KMt = a.f32(1024), a.f32(1024), a.f32(1024), a.f32(1024)
    twb = a.bf16(256)
    twT = a.bf16(256)
    n2 = a.f32(16)
    bon = a.f32(32).rearrange("p (d h) -> p d h", d=2)
    bsum = a.f32(16)

    def g3(x):
        return x.rearrange("p (h j) -> p h j", j=64)

    allct = list(range(9))
    for tt in range(TT):
        r0 = tt * 128
        rd = [("pB", t_, c_) for t_ in (tt - 1, tt, tt + 1) if 0 <= t_ < TT for c_ in range(7)]
        p.dma("sync", P0, k.pB_d[r0:r0 + 128, 0:3328], reads=rd, writes=["P0"])
        if tt == 0:
            p.op("vector", lambda E: E.memset(Pp, 0.0), writes=["Pp"])
        else:
            p.dma("sync", Pp[0:1, :], k.pB_d[r0 - 1:r0, 0:3328], reads=rd, writes=["Pp0"])
        p.dma("sync", Pp[1:128, :], k.pB_d[r0:r0 + 127, 0:3328], reads=rd + ["Pp"], writes=["Pp1"])
        if tt == TT - 1:
            p.op("vector", lambda E: E.memset(Pn, 0.0), reads=["Pn0", "Pn1"], writes=["Pn"])
        else:
            p.dma("sync", Pn[127:128, :], k.pB_d[r0 + 128:r0 + 129, 0:3328], reads=rd, writes=["Pn0"])
        p.dma("sync", Pn[0:127, :], k.pB_d[r0 + 1:r0 + 128, 0:3328], reads=rd + ["Pn"], writes=["Pn1"])
        PpK, PnK = ["Pp", "Pp0", "Pp1"], ["Pn", "Pn0", "Pn1"]
        p.op("gpsimd", lambda E: E.tensor_tensor(out=Pp, in0=Pp, in1=P0, op=ALU.subtract), reads=PpK + ["P0"], writes=PpK)
        p.op("vector", lambda E: E.tensor_tensor(out=Pn, in0=Pn, in1=P0, op=ALU.subtract), reads=PnK + ["P0"], writes=PnK)
        p.op("gpsimd", lambda E: E.tensor_tensor(out=Pp, in0=Pp, in1=mu0b, op=ALU.mult), reads=PpK + [prm[0]], writes=PpK)
        p.op("vector", lambda E: E.tensor_tensor(out=Pn, in0=Pn, in1=mu1b, op=ALU.mult), reads=PnK + [prm[1]], writes=PnK)
        p.op("vector", lambda E: E.tensor_tensor(out=P0, in0=P0, in1=Pp, op=ALU.add), reads=PpK + ["P0"], writes=["P0"])
        p.op("vector", lambda E: E.tensor_tensor(out=P0, in0=P0, in1=Pn, op=ALU.add), reads=PnK + ["P0"], writes=["P0"])
        r_, kx, v_ = P0[:, 0:1024], P0[:, 1024:2048], P0[:, 2048:3072]
        p.dma("sync", rw["R"][r0:r0 + 128, :], r_, reads=["P0"], writes=[("rwR", tt)])
        p.dma("sync", rw["V"][r0:r0 + 128, :], v_, reads=["P0"], writes=[("rwV", tt)])
        p.op("scalar", lambda E: E.activation(out=twb[:, 0:128], in_=P0[:, 3072:3200], func=AF.Tanh), reads=["P0"],
             writes=["twb"])
        p.op("scalar", lambda E: E.activation(out=twb[:, 128:256], in_=P0[:, 3200:3328], func=AF.Copy), reads=["P0"],
             writes=["twb"])
        b = k.next_bank()
        psb = k.ps[b].bitcast(BF16)
        for i in range(2):
            p.op("tensor", lambda E, i=i, psb=psb: E.transpose(out=psb[:, i * 128:(i + 1) * 128],
                                                                in_=twb[:, i * 128:(i + 1) * 128], identity=k.ident_bf),
                 reads=["twb", "ident"], writes=[("ps", b)])
        p.op("vector", lambda E, psb=psb: E.tensor_copy(out=twT, in_=psb[:, 0:256]), reads=[("ps", b)], writes=["twT"])
        p.op("vector", lambda E: E.tensor_tensor(out=KKt, in0=kx, in1=k_kb, op=ALU.mult), reads=["P0", prm[4]],
             writes=["KKt"])
        p.op("gpsimd", lambda E: E.tensor_tensor(out=tmp, in0=KKt, in1=KKt, op=ALU.mult), reads=["KKt"], writes=["tmp"])
        p.op("vector", lambda E: E.tensor_reduce(out=n2, in_=g3(tmp), axis=AX.X, op=ALU.add), reads=["tmp"],
             writes=["n2"])
        p.op("scalar", lambda E: E.activation(out=n2, in_=n2, func=AF.Sqrt), reads=["n2"], writes=["n2"])
        p.op("vector", lambda E: E.tensor_scalar(out=n2, in0=n2, scalar1=1e-12, scalar2=None, op0=ALU.max),
             reads=["n2"], writes=["n2"])
        p.op("vector", lambda E: E.reciprocal(out=n2, in_=n2), reads=["n2"], writes=["n2"])
        p.op("vector", lambda E: E.tensor_tensor(out=g3(KKt), in0=g3(KKt), in1=bc_last(n2, 64), op=ALU.mult),
             reads=["KKt", "n2"], writes=["KKt"])
        p.dma("sync", rw["KK"][r0:r0 + 128, :], KKt, reads=["KKt"], writes=[("rwKK", tt)])
        p.op("gpsimd", lambda E: E.tensor_tensor(out=RK, in0=r_, in1=r_kb, op=ALU.mult), reads=["P0", prm[6]],
             writes=["RK"])
        for d in range(2):
            dsl = slice(d * 64, (d + 1) * 64)
            for (dstT, lhs_cols, upw, biasb, pk) in ((Wt, slice(0, 128), wup, w0b, prm[7]),
                                                      (At, slice(128, 256), aup, a0b, prm[8])):
                for hf in range(2):
                    b = k.next_bank()
                    cs = slice(hf * 512, (hf + 1) * 512)
                    p.op("tensor", lambda E, b=b, dsl=dsl, lhs_cols=lhs_cols, upw=upw, cs=cs: E.matmul(
                        k.ps[b], lhsT=twT[dsl, lhs_cols], rhs=upw[dsl, cs], start=True, stop=True),
                        reads=["twT", pk], writes=[("ps", b)])
                    dk = "Wt" if dstT is Wt else "At"
                    p.op("vector", lambda E, b=b, dstT=dstT, cs=cs, biasb=biasb, d=d: E.tensor_tensor(
                        out=dstT[:, cs], in0=k.ps[b], in1=biasb[:, d, cs], op=ALU.add),
                        reads=[("ps", b), prm[2], prm[3]], writes=[dk])
            p.op("scalar", lambda E: E.activation(out=Wt, in_=Wt, func=AF.Sigmoid), reads=["Wt"], writes=["Wt"])
            p.op("scalar", lambda E: E.activation(out=Wt, in_=Wt, func=AF.Exp, scale=-math.exp(-0.5)), reads=["Wt"],
                 writes=["Wt"])
            p.dma("sync", rw["W%d" % d][r0:r0 + 128, :], Wt, reads=["Wt"], writes=[("rwW", d, tt)])
            p.op("scalar", lambda E: E.activation(out=At, in_=At, func=AF.Sigmoid), reads=["At"], writes=["At"])
            p.op("vector", lambda E: E.tensor_tensor(out=KAt, in0=KKt, in1=At, op=ALU.mult), reads=["KKt", "At"],
                 writes=["KAt"])
            p.dma("sync", rw["KA%d" % d][r0:r0 + 128, :], KAt, reads=["KAt"], writes=[("rwKA", d, tt)])
            p.op("vector", lambda E: E.scalar_tensor_tensor(out=At, in0=At, scalar=-1.0, in1=k_ab, op0=ALU.add,
                                                            op1=ALU.mult), reads=["At", prm[5]], writes=["At"])
            p.op("vector", lambda E: E.scalar_tensor_tensor(out=KMt, in0=At, scalar=1.0, in1=kx, op0=ALU.add,
                                                            op1=ALU.mult), reads=["At", "P0"], writes=["KMt"])
            p.dma("sync", rw["KM%d" % d][r0:r0 + 128, :], KMt, reads=["KMt"], writes=[("rwKM", d, tt)])
            p.op("gpsimd", lambda E: E.tensor_tensor(out=tmp, in0=RK, in1=KMt, op=ALU.mult), reads=["RK", "KMt"],
                 writes=["tmp"])
            p.op("vector", lambda E, d=d: E.tensor_reduce(out=bon[:, d, :], in_=g3(tmp), axis=AX.X, op=ALU.add),
                 reads=["tmp"], writes=[("bon", d)])
        p.op("vector", lambda E: E.tensor_tensor(out=bsum, in0=bon[:, 0, :], in1=bon[:, 1, :], op=ALU.add),
             reads=[("bon", 0), ("bon", 1)], writes=["bsum"])
        p.dma("sync", k.bon_d[r0:r0 + 128, :], bsum, reads=["bsum"], writes=[("bon_d", tt)])
    p.barrier()
    a.release(m0)
    m0 = a.mark()
    a2 = Arena(k.uT_f32, 16 * S // 2)
    TC = 32
    NCH = S // TC
    names = ["KK", "W", "KA", "KM", "R"]
    bufs = [{n: a.f32(TC * 64).rearrange("p (t j) -> p t j", j=64) for n in names} for _ in range(2)]
    vb = [a.f32(TC * 16).rearrange("p (t i) -> p t i", i=16) for _ in range(2)]
    yb = [a.f32(TC * 16).rearrange("p (t i) -> p t i", i=16) for _ in range(2)]
    St = a2.f32(1024)
    Sw = a2.f32(1024)
    t1 = a2.f32(1024)
    vk = a2.f32(1024)
    sa = a2.f32(16)

    def s3(x):
        return x.rearrange("p (i j) -> p i j", j=64)

    p.op("vector", lambda E: E.memset(St, 0.0), writes=["St"])
    allpre = ([("rwR", t_) for t_ in range(TT)] + [("rwV", t_) for t_ in range(TT)] + [("rwKK", t_) for t_ in range(TT)]
              + [(n_, d_, t_) for n_ in ("rwW", "rwKA", "rwKM") for d_ in range(2) for t_ in range(TT)])
    qi = 0
    for c in range(NCH):
        bi = c % 2
        B = bufs[bi]
        t0 = c * TC
        for d in range(2):
            tstart = t0 if d == 0 else S - 1 - t0
            tstep = 1024 if d == 0 else -1024
            for n in names:
                src_t = k.rw[n] if n in ("KK", "R") else k.rw["%s%d" % (n, d)]
                for ig in range(4):
                    src = bass.AP(src_t, tstart * 1024, [[64, 16], [tstep, TC], [1, 64]])
                    dst = pslice(B[n], d * 64 + ig, 16, 4)
                    p.dma("sync" if qi % 2 == 0 else "scalar", dst, src, reads=allpre if c == 0 else [],
                          writes=[("sc", bi, n, d, ig)])
                    qi += 1
            srcv = bass.AP(k.rw["V"], tstart * 1024, [[16, 64], [tstep, TC], [1, 16]])
            p.dma("sync" if qi % 2 == 0 else "scalar", vb[bi][d * 64:(d + 1) * 64], srcv,
                  reads=allpre if c == 0 else [], writes=[("scv", bi, d)])
            qi += 1
        rk_ = {n: [("sc", bi, n, d, ig) for d in range(2) for ig in range(4)] for n in names}
        vkeys = [("scv", bi, 0), ("scv", bi, 1)]
        for tau in range(TC):
            kkb = bc_mid(B["KK"][:, tau, :], 16)
            wb = bc_mid(B["W"][:, tau, :], 16)
            kab = bc_mid(B["KA"][:, tau, :], 16)
            kmb = bc_mid(B["KM"][:, tau, :], 16)
            rb = bc_mid(B["R"][:, tau, :], 16)
            vbb = bc_last(vb[bi][:, tau, :], 64)
            p.op("vector", lambda E, kkb=kkb: E.tensor_tensor(out=s3(t1), in0=s3(St), in1=kkb, op=ALU.mult),
                 reads=["St"] + rk_["KK"], writes=["t1"])
            p.op("gpsimd", lambda E, wb=wb: E.tensor_tensor(out=s3(Sw), in0=s3(St), in1=wb, op=ALU.mult),
                 reads=["St"] + rk_["W"], writes=["Sw"])
            p.op("vector", lambda E: E.tensor_reduce(out=sa, in_=s3(t1), axis=AX.X, op=ALU.add), reads=["t1"],
                 writes=["sa"])
            p.op("gpsimd", lambda E, vbb=vbb, kmb=kmb: E.tensor_tensor(out=s3(vk), in0=vbb, in1=kmb, op=ALU.mult),
                 reads=vkeys + rk_["KM"], writes=["vk"])
            p.op("vector", lambda E, kab=kab: E.tensor_tensor(out=s3(t1), in0=bc_last(sa, 64), in1=kab, op=ALU.mult),
                 reads=["sa"] + rk_["KA"], writes=["t1"])
            p.op("gpsimd", lambda E: E.tensor_tensor(out=Sw, in0=Sw, in1=vk, op=ALU.add), reads=["Sw", "vk"],
                 writes=["Sw"])
            p.op("vector", lambda E: E.tensor_tensor(out=St, in0=Sw, in1=t1, op=ALU.subtract), reads=["Sw", "t1"],
                 writes=["St"])
            p.op("vector", lambda E, rb=rb: E.tensor_tensor(out=s3(t1), in0=s3(St), in1=rb, op=ALU.mult),
                 reads=["St"] + rk_["R"], writes=["t1"])
            p.op("vector", lambda E, tau=tau, bi=bi: E.tensor_reduce(out=yb[bi][:, tau, :], in_=s3(t1), axis=AX.X,
                                                                     op=ALU.add), reads=["t1"], writes=[("yb", bi)])
        for d in range(2):
            tstart = t0 if d == 0 else S - 1 - t0
            tstep = 1024 if d == 0 else -1024
            dsty = bass.AP(k.rw["Y%d" % d], tstart * 1024, [[16, 64], [tstep, TC], [1, 16]])
            p.dma("sync", dsty, yb[bi][d * 64:(d + 1) * 64], reads=[("yb", bi)], writes=[("rwY", d, c)])
    p.barrier()
    a.release(m0)
    m0 = a.mark()
    a2 = Arena(k.uT_f32, 16 * S // 2)
    lgb, lbb = a2.f32(1024), a2.f32(1024)
    p.dma("sync", lgb, k.din["rwkv_lnx_g"][l].partition_broadcast(128), writes=["lgb"])
    p.dma("sync", lbb, k.din["rwkv_lnx_b"][l].partition_broadcast(128), writes=["lbb"])
    Y0t, Y1t, vt, bgt, sqt = a.f32(1024), a.f32(1024), a.f32(1024), a.f32(1024), a.f32(1024)
    bont = a.f32(16)
    mean = a.f32(16)
    var = a.f32(16)
    obt = a.bf16(1024)
    mixTt = a.bf16(1024).rearrange("p (c t) -> p c t", c=8)
    ally = [("rwY", d, c) for d in range(2) for c in range(NCH)]
    for tt in range(TT):
        r0 = tt * 128
        p.dma("sync", Y0t, rw["Y0"][r0:r0 + 128, :], reads=ally if tt == 0 else [], writes=["Y0t"])
        p.dma("sync", Y1t, rw["Y1"][r0:r0 + 128, :], reads=ally if tt == 0 else [], writes=["Y1t"])
        p.dma("sync", vt, rw["V"][r0:r0 + 128, :], writes=["vt"])
        p.dma("sync", bgt, k.pB_d[r0:r0 + 128, 3328:4352], writes=["bgt"])
        p.dma("sync", bont, k.bon_d[r0:r0 + 128, :], writes=["bont"])
        p.op("vector", lambda E: E.tensor_tensor(out=Y0t, in0=Y0t, in1=Y1t, op=ALU.add), reads=["Y0t", "Y1t"],
             writes=["Y0t"])
        p.op("vector", lambda E: E.tensor_reduce(out=mean, in_=g3(Y0t), axis=AX.X, op=ALU.add), reads=["Y0t"],
             writes=["mean"])
        p.op("vector", lambda E: E.tensor_scalar(out=mean, in0=mean, scalar1=1.0 / 64, scalar2=None, op0=ALU.mult),
             reads=["mean"], writes=["mean"])
        p.op("vector", lambda E: E.tensor_tensor(out=g3(Y0t), in0=g3(Y0t), in1=bc_last(mean, 64), op=ALU.subtract),
             reads=["Y0t", "mean"], writes=["Y0t"])
        p.op("gpsimd", lambda E: E.tensor_tensor(out=sqt, in0=Y0t, in1=Y0t, op=ALU.mult), reads=["Y0t"], writes=["sqt"])
        p.op("vector", lambda E: E.tensor_reduce(out=var, in_=g3(sqt), axis=AX.X, op=ALU.add), reads=["sqt"],
             writes=["var"])
        p.op("scalar", lambda E: E.activation(out=var, in_=var, func=AF.Sqrt, bias=k.eps_t[:, 2:3], scale=1.0 / 64),
             reads=["var", "eps"], writes=["var"])
        p.op("vector", lambda E: E.reciprocal(out=var, in_=var), reads=["var"], writes=["var"])
        p.op("vector", lambda E: E.tensor_tensor(out=g3(Y0t), in0=g3(Y0t), in1=bc_last(var, 64), op=ALU.mult),
             reads=["Y0t", "var"], writes=["Y0t"])
        p.op("gpsimd", lambda E: E.tensor_tensor(out=Y0t, in0=Y0t, in1=lgb, op=ALU.mult), reads=["Y0t", "lgb"],
             writes=["Y0t"])
        p.op("vector", lambda E: E.tensor_tensor(out=Y0t, in0=Y0t, in1=lbb, op=ALU.add), reads=["Y0t", "lbb"],
             writes=["Y0t"])
        p.op("gpsimd", lambda E: E.tensor_tensor(out=g3(vt), in0=g3(vt), in1=bc_last(bont, 64), op=ALU.mult),
             reads=["vt", "bont"], writes=["vt"])
        p.op("vector", lambda E: E.tensor_tensor(out=Y0t, in0=Y0t, in1=vt, op=ALU.add), reads=["Y0t", "vt"],
             writes=["Y0t"])
        p.op("scalar", lambda E: E.activation(out=bgt, in_=bgt, func=AF.Silu), reads=["bgt"], writes=["bgt"])
        p.op("vector", lambda E: E.tensor_tensor(out=obt, in0=Y0t, in1=bgt, op=ALU.mult), reads=["Y0t", "bgt"],
             writes=["obt"])
        b = k.next_bank()
        psb = k.ps[b].bitcast(BF16)
        for cc in range(8):
            p.op("tensor", lambda E, cc=cc, psb=psb: E.transpose(out=psb[:, cc * 128:(cc + 1) * 128],
                                                                  in_=obt[:, cc * 128:(cc + 1) * 128],
                                                                  identity=k.ident_bf),
                 reads=["obt", "ident"], writes=[("ps", b)])
        p.op("vector", lambda E, psb=psb: E.tensor_copy(out=mixTt.rearrange("p c t -> p (c t)"), in_=psb[:, 0:1024]),
             reads=[("ps", b)], writes=["mixTt"])
        p.dma("sync", k.mixedT_d[1024:2048, r0:r0 + 128].rearrange("(c p) t -> p c t", p=128), mixTt, reads=["mixTt"],
              writes=[("mixedT", 1024 + cc * 128) for cc in range(8)])
    p.barrier()
    a.release(m0)


def build(dbg=None):
    dbg = dbg or {}
    nc = bass.Bass("TRN2", target_bir_lowering=False)
    k = K()
    k.nc = nc
    k.dbg = dbg
    din = {}
    dh = {}

    def inp(name, shape, dt=F32):
        dh[name] = nc.dram_tensor(name, list(shape), dt, kind="ExternalInput")
        din[name] = dh[name].ap()

    inp("x", [S, D])
    inp("rel_bias", [32, 8])
    inp("pre_norm_g", [NL, D])
    inp("post_norm_g", [NL, D])
    inp("w_in", [NL, D, PW])
    inp("w_out", [NL, MW, D])
    for n in ["lambda_q1", "lambda_k1", "lambda_q2", "lambda_k2"]:
        inp(n, [NL, 64])
    inp("subln_g", [NL, 128])
    inp("rwkv_shift_mu", [NL, 2, 3328])
    inp("rwkv_w0", [NL, 2, 1024])
    inp("rwkv_w_up", [NL, 2, 64, 1024])
    inp("rwkv_a0", [NL, 2, 1024])
    inp("rwkv_a_up", [NL, 2, 64, 1024])
    for n in ["rwkv_k_k", "rwkv_k_a", "rwkv_r_k", "rwkv_lnx_g", "rwkv_lnx_b"]:
        inp(n, [NL, 1024])
    inp("hgrn_lb_logits", [2, NL, 1024])
    inp("hgrn_norm_g", [NL, 128])
    inp("c_ident", [128, 128])
    inp("c_J", [128, 128])
    inp("c_oh", [32, 1280])
    inp("c_maskF", [128, 128])
    inp("c_maskB", [128, 128])
    inp("c_cm", [128, 3 * S])
    k.din, k.dh = din, dh
    y = nc.dram_tensor("y", [S, D], F32, kind="ExternalOutput").ap()
    k.y = y
    dout = {}
    for name, (shape, dt) in dbg.get("outs", {}).items():
        dout[name] = nc.dram_tensor(name, list(shape), dt, kind="ExternalOutput").ap()
    k.dout = dout
    k.mixedT_h = nc.dram_tensor("mixedT_d", [MW, S], BF16)
    k.mixedT_d = k.mixedT_h.ap()
    k.x1_d = nc.dram_tensor("x1_d", [S, D], F32).ap()
    k.F_h = nc.dram_tensor("F_d", [8, 1280], F32)
    k.pB_d = nc.dram_tensor("pB_d", [S, 4352], F32).ap()
    k.rw = {}
    for n in ["R", "V", "KK", "W0", "W1", "KA0", "KA1", "KM0", "KM1", "Y0", "Y1"]:
        k.rw[n] = nc.dram_tensor("rw_" + n, [S, 1024], F32)
    k.bon_d = nc.dram_tensor("bon_d", [S, 16], F32).ap()

    NW = 51 * 1024 + 512
    with ExitStack() as es:
        es.enter_context(nc.allow_non_contiguous_dma(reason="small param layouts"))
        arena_t = es.enter_context(nc.sbuf_tensor("arena", [128, NW], F32))
        k.arena = Arena(arena_t, NW)
        k.ps = [es.enter_context(nc.psum_tensor(f"psb{i}", [128, 512], F32))[:, :] for i in range(8)]
        k.bank = 0

        def next_bank():
            b = k.bank
            k.bank = (k.bank + 1) % 8
            return b
        k.next_bank = next_bank
        k.p = Prog(nc, es)
        p, a = k.p, k.arena
        ident_f = a.f32(128)
        k.ident_f = ident_f
        p.dma("sync", ident_f, din["c_ident"], writes=["ident_f"])
        k.ident_bf = a.bf16(128)
        p.op("vector", lambda E: E.tensor_copy(out=k.ident_bf, in_=ident_f), reads=["ident_f"], writes=["ident"])
        k.J = a.f32(128)
        p.dma("sync", k.J, din["c_J"], writes=["J"])
        k.eps_t = a.f32(4)
        p.op("vector", lambda E: E.memset(k.eps_t[:, 0:1], EPS), writes=["eps"])
        p.op("vector", lambda E: E.memset(k.eps_t[:, 1:2], 1e-5), writes=["eps"])
        p.op("vector", lambda E: E.memset(k.eps_t[:, 2:3], 64e-5), writes=["eps"])
        p.op("vector", lambda E: E.memset(k.eps_t[:, 3:4], 0.0), writes=["eps"])
        stages = dbg.get("stages", "PACBO")
        layers = dbg.get("layers", list(range(NL)))
        if "F" in stages or "A" in stages:
            setup_bias_table(k)
        k.uT_f32 = a.f32(16 * S // 2)
        k.uT = k.uT_f32.bitcast(BF16).rearrange("p (c t) -> p c t", c=16)
        for l in layers:
            xsrc = din["x"] if l == 0 else k.x1_d
            ydst = k.x1_d if l < NL - 1 else y
            if dbg.get("single_layer_out"):
                ydst = y
            if "P" in stages:
                stage_prenorm(k, l, xsrc)
            if "A" in stages:
                stage_attn(k, l, dbg.get("heads_a", range(8)))
            if "C" in stages:
                stage_hgrn(k, l, dbg.get("heads_c", range(8)))
            if "B" in stages:
                stage_rwkv(k, l)
            if "O" in stages:
                stage_out(k, l, xsrc, ydst)
        for name, fn in dbg.get("dumps", {}).items():
            fn(k, dout[name])
        p.finish()
        p.emit()
    return nc


def t5_bucket_np(rel):
    n = np.abs(rel)
    nf = np.maximum(n, 8).astype(np.float32)
    large = 8 + (np.log(nf / 8) / math.log(16) * 8).astype(np.int32)
    large = np.minimum(large, 15)
    return np.where(rel > 0, 16, 0) + np.where(n < 8, n, large)


def consts():
    c = {"c_ident": np.eye(128, dtype=np.float32)}
    c["c_J"] = np.ascontiguousarray(np.eye(128, dtype=np.float32)[::-1])
    r = np.arange(1280)
    bk = t5_bucket_np(639 - r)
    oh = np.zeros((32, 1280), np.float32)
    oh[bk, r] = 1.0
    c["c_oh"] = oh
    s_ = np.arange(128)[:, None]
    t_ = np.arange(128)[None, :]
    same = (s_ // 64) == (t_ // 64)
    c["c_maskF"] = (same & (s_ <= t_)).astype(np.float32)
    c["c_maskB"] = (same & (s_ >= t_)).astype(np.float32)
    t = np.arange(S)
    cm = np.zeros((3, S), np.float32)
    cm[0] = (t % 64 != 0)
    cm[1] = ((t // 64) % 2 == 0)
    cm[2] = ((t // 64) % 2 == 1)
    c["c_cm"] = np.ascontiguousarray(np.broadcast_to(cm.reshape(1, 3 * S), (128, 3 * S)))
    return c


def kernel(**inputs):
    n = 8
    nc = build()
    c = consts()
    in_maps = []
    for b in range(n):
        m = {kk: np.ascontiguousarray(v) for kk, v in inputs.items() if kk != "x"}
        m["x"] = np.ascontiguousarray(inputs["x"][b])
        m.update(c)
        in_maps.append(m)
    res = run_bass_kernel_spmd(nc, in_maps, core_ids=list(range(n)))
    return np.stack([np.asarray(r["y"]) for r in res.results], axis=0).astype(np.float32)
```

```python
import math
from contextlib import ExitStack
import numpy as np
import concourse.bass as bass
import concourse.mybir as mybir
from concourse.bass_utils import run_bass_kernel_spmd

F32 = mybir.dt.float32
BF16 = mybir.dt.bfloat16
AF = mybir.ActivationFunctionType
ALU = mybir.AluOpType
AX = mybir.AxisListType

S = 2048
D = 2048
PW = 13568
MW = 3072
NL = 2
EPS = 1e-6

ENGS = ["sync", "scalar", "vector", "gpsimd", "tensor"]
SEM_CAP = 30000
N_DMA_SEMS = 24


class Prog:
    def __init__(self, nc, es):
        self.nc = nc
        self.es = es
        self.q = {e: [] for e in ENGS}
        self.cur = {}
        self.waited = {e: {} for e in ENGS}
        self.lw = {}
        self.rd = {}
        self.nsem = 0
        self.dma_sems = []
        self.dma_state = []
        self.dma_rr = 0
        self.n_ins = {e: 0 for e in ENGS}
        for e in ["scalar", "vector", "gpsimd", "tensor"]:
            self._new_phase(e)
        for i in range(N_DMA_SEMS):
            s = es.enter_context(nc.semaphore(f"dma{i}"))
            self.dma_sems.append(s)
            self.dma_state.append(0)

    def _new_phase(self, e):
        s = self.es.enter_context(self.nc.semaphore(f"s_{e}_{self.nsem}"))
        self.nsem += 1
        self.cur[e] = [s, 0]

    def _need(self, eng, deps):
        w = self.waited[eng]
        best = {}
        for (s, v) in deps:
            if v <= w.get(id(s), 0):
                continue
            if v > best.get(id(s), (None, 0))[1]:
                best[id(s)] = (s, v)
        for sid, (s, v) in best.items():
            w[sid] = v
            self.q[eng].append(lambda E, s=s, v=v: E.wait_ge(s, v))

    def _deps(self, eng, reads, writes):
        deps = []
        own = self.cur[eng][0] if eng in self.cur else None
        for k in list(reads) + list(writes):
            x = self.lw.get(k)
            if x is not None:
                deps.append(x)
        for k in writes:
            for s, v in self.rd.get(k, {}).values():
                deps.append((s, v))
        if eng == "tensor":
            deps = [d for d in deps if d[0] is not own]
        return deps

    def _commit(self, tok, reads, writes):
        s, v = tok
        for k in writes:
            self.lw[k] = tok
            self.rd[k] = {}
        for k in reads:
            if k in writes:
                continue
            d = self.rd.setdefault(k, {})
            if d.get(id(s), (None, 0))[1] < v:
                d[id(s)] = (s, v)

    def op(self, eng, fn, reads=(), writes=()):
        deps = self._deps(eng, reads, writes)
        self._need(eng, deps)
        c = self.cur[eng]
        if c[1] >= SEM_CAP:
            self._new_phase(eng)
            c = self.cur[eng]
        c[1] += 1
        s, v = c[0], c[1]
        self.q[eng].append(lambda E, s=s: fn(E).then_inc(s, 1))
        self.n_ins[eng] += 1
        self._commit((s, v), reads, writes)

    def dma(self, eng, out, in_, reads=(), writes=(), **kw):
        i = self.dma_rr
        self.dma_rr = (self.dma_rr + 1) % N_DMA_SEMS
        s = self.dma_sems[i]
        deps = self._deps(eng, reads, writes)
        if self.dma_state[i] > 0:
            deps.append((s, self.dma_state[i]))
        self._need(eng, deps)
        self.dma_state[i] += 16
        v = self.dma_state[i]
        self.q[eng].append(lambda E, s=s: E.dma_start(out=out, in_=in_, **kw).then_inc(s, 16))
        self.n_ins[eng] += 1
        self._commit((s, v), reads, writes)

    def all_tokens(self):
        deps = [(s, v) for s, v in zip(self.dma_sems, self.dma_state) if v > 0]
        for e in self.cur:
            if self.cur[e][1] > 0:
                deps.append((self.cur[e][0], self.cur[e][1]))
        return deps

    def barrier(self):
        deps = self.all_tokens()
        for e in ENGS:
            self._need(e, deps)

    def finish(self):
        self._need("sync", self.all_tokens())

    def emit(self):
        with self.nc.Block() as block:
            for e in ENGS:
                lst = self.q[e]
                if not lst:
                    continue

                def body(E, lst=lst):
                    for f in lst:
                        f(E)
                getattr(block, e)(body)


class Arena:
    def __init__(self, t, nwords):
        self.t = t
        self.n = nwords
        self.off = 0

    def mark(self):
        return self.off

    def release(self, m):
        self.off = m

    def f32(self, n):
        o = self.off
        self.off += n
        assert self.off <= self.n, ("SBUF arena overflow", self.off, self.n)
        return self.t[:, o:o + n]

    def bf16(self, n):
        w = (n + 1) // 2
        return self.f32(w).bitcast(BF16)[:, 0:n]


class K:
    pass


def load_row_fm(k, dst, src_vec, eng="sync", key=None):
    k.p.dma(eng, dst, src_vec.rearrange("(c p) -> p c", p=128), writes=[key])


def stage_prenorm(k, l, xsrc):
    p, a = k.p, k.arena
    m = a.mark()
    g = a.f32(16)
    load_row_fm(k, g, k.din["pre_norm_g"][l], key="pn_g")
    xb = [a.f32(D) for _ in range(2)]
    xn = a.bf16(4 * D).rearrange("p (i d) -> p i d", i=4)
    junk = a.bf16(D)
    ss = a.f32(16)
    rstd = a.f32(16)
    p.op("vector", lambda E: E.memset(ss, 0.0), writes=["pn_ss"])
    ev = 0
    for tg in range(4):
        for ti in range(4):
            tt = tg * 4 + ti
            xt = xb[tt % 2]
            xk = ("pn_x", tt % 2)
            p.dma("sync", xt, xsrc[tt * 128:(tt + 1) * 128, :], writes=[xk])
            p.op("scalar", lambda E, xt=xt, tt=tt: E.activation(out=junk, in_=xt, func=AF.Square,
                                                                  accum_out=ss[:, tt:tt + 1]),
                 reads=[xk, "pn_ss"], writes=["pn_junk", ("pn_ss", tt)])
            p.op("scalar", lambda E, tt=tt: E.activation(out=rstd[:, tt:tt + 1], in_=ss[:, tt:tt + 1], func=AF.Sqrt,
                                                          bias=k.eps_t[:, 0:1], scale=1.0 / D),
                 reads=[("pn_ss", tt), "pn_ss", "eps"], writes=[("pn_r", tt)])
            p.op("vector", lambda E, tt=tt: E.reciprocal(out=rstd[:, tt:tt + 1], in_=rstd[:, tt:tt + 1]),
                 reads=[("pn_r", tt)], writes=[("pn_r", tt)])
            p.op("scalar", lambda E, xt=xt, tt=tt, ti=ti: E.activation(out=xn[:, ti, :], in_=xt, func=AF.Copy,
                                                                         scale=rstd[:, tt:tt + 1]),
                 reads=[xk, ("pn_r", tt)], writes=[("pn_xn", ti)])
        for kc in range(16):
            b = k.next_bank()
            psb = k.ps[b].bitcast(BF16)
            for ti in range(4):
                p.op("tensor", lambda E, ti=ti, kc=kc, psb=psb: E.transpose(
                    out=psb[:, ti * 128:(ti + 1) * 128], in_=xn[:, ti, kc * 128:(kc + 1) * 128], identity=k.ident_bf),
                    reads=[("pn_xn", ti), "ident"], writes=[("ps", b)])
            dst = k.uT[:, kc, tg * 512:(tg + 1) * 512]
            if ev % 2 == 0:
                p.op("vector", lambda E, dst=dst, psb=psb, kc=kc: E.tensor_scalar(
                    out=dst, in0=psb[:, 0:512], scalar1=g[:, kc:kc + 1], scalar2=None, op0=ALU.mult),
                    reads=[("ps", b), "pn_g"], writes=[("uT", kc, tg)])
            else:
                p.op("scalar", lambda E, dst=dst, psb=psb, kc=kc: E.activation(
                    out=dst, in_=psb[:, 0:512], func=AF.Copy, scale=g[:, kc:kc + 1]),
                    reads=[("ps", b), "pn_g"], writes=[("uT", kc, tg)])
            ev += 1
    p.barrier()
    a.release(m)


def bc_last(ap2, n):
    return ap2.unsqueeze(2).broadcast_to([ap2.shape[0], ap2.shape[1], n])


def bc_mid(ap2, n):
    return ap2.unsqueeze(1).broadcast_to([ap2.shape[0], n, ap2.shape[1]])


def load_w(k, wt, l, segs, key):
    off = 0
    keys = []
    for i, (c0, n) in enumerate(segs):
        kk = (key, i)
        k.p.dma("gpsimd", wt[:, :, off:off + n],
                k.din["w_in"][l][:, c0:c0 + n].rearrange("(c p) n -> p c n", p=128), writes=[kk])
        keys.append(kk)
        off += n
    return keys


def evac_copy(k, i, out, in_, reads, writes):
    if i % 2 == 0:
        k.p.op("vector", lambda E: E.tensor_copy(out=out, in_=in_), reads=reads, writes=writes)
    else:
        k.p.op("scalar", lambda E: E.activation(out=out, in_=in_, func=AF.Copy), reads=reads, writes=writes)


def epilogue(k, o3, gain, gate3, eps_ap, inv_n, mrow0, tag):
    p, a = k.p, k.arena
    m = a.mark()
    sq = a.f32(2048).rearrange("p (a b) -> p a b", a=16)
    ssq = a.f32(16)
    ob = a.bf16(2048).rearrange("p (a b) -> p a b", a=16)
    mixT = a.bf16(2048)
    ko = (tag, "o")
    p.op("gpsimd", lambda E: E.tensor_tensor(out=sq, in0=o3, in1=o3, op=ALU.mult), reads=[ko], writes=[(tag, "sq")])
    p.op("vector", lambda E: E.tensor_reduce(out=ssq, in_=sq, axis=AX.X, op=ALU.add), reads=[(tag, "sq")],
         writes=[(tag, "ssq")])
    p.op("scalar", lambda E: E.activation(out=ssq, in_=ssq, func=AF.Sqrt, bias=eps_ap, scale=inv_n),
         reads=[(tag, "ssq"), "eps"], writes=[(tag, "ssq")])
    p.op("vector", lambda E: E.reciprocal(out=ssq, in_=ssq), reads=[(tag, "ssq")], writes=[(tag, "ssq")])
    p.op("vector", lambda E: E.tensor_tensor(out=o3, in0=o3, in1=bc_last(ssq, 128), op=ALU.mult),
         reads=[ko, (tag, "ssq")], writes=[ko])
    p.op("gpsimd", lambda E: E.tensor_tensor(out=o3, in0=o3, in1=bc_mid(gain, 16), op=ALU.mult),
         reads=[ko, (tag, "gain")], writes=[ko])
    p.op("vector", lambda E: E.tensor_tensor(out=ob, in0=o3, in1=gate3, op=ALU.mult),
         reads=[ko, (tag, "gate")], writes=[(tag, "ob")])
    for half in range(2):
        b = k.next_bank()
        psb = k.ps[b].bitcast(BF16)
        for i in range(8):
            tt = half * 8 + i
            p.op("tensor", lambda E, i=i, tt=tt, psb=psb: E.transpose(out=psb[:, i * 128:(i + 1) * 128],
                                                                     in_=ob[:, tt, :], identity=k.ident_bf),
                 reads=[(tag, "ob"), "ident"], writes=[("ps", b)])
        evac_copy(k, half, mixT[:, half * 1024:(half + 1) * 1024], psb[:, 0:1024], [("ps", b)], [(tag, "mixT", half)])
    p.dma("sync", k.mixedT_d[mrow0:mrow0 + 128, :], mixT, reads=[(tag, "mixT", 0), (tag, "mixT", 1)],
          writes=[("mixedT", mrow0)])
    a.release(m)


def setup_bias_table(k):
    p, a = k.p, k.arena
    m = a.mark()
    rb = a.f32(8)
    oh = a.f32(1280)
    fsb = a.f32(1280)
    p.dma("sync", rb[0:32, :], k.din["rel_bias"], writes=["rb"])
    p.dma("sync", oh[0:32, :], k.din["c_oh"], writes=["oh"])
    for c in range(3):
        n = 512 if c < 2 else 256
        b = k.next_bank()
        p.op("tensor", lambda E, b=b, c=c, n=n: E.matmul(k.ps[b][0:8, 0:n], lhsT=rb[0:32, :],
                                                         rhs=oh[0:32, c * 512:c * 512 + n], start=True, stop=True),
             reads=["rb", "oh"], writes=[("ps", b)])
        p.op("vector", lambda E, b=b, c=c, n=n: E.tensor_copy(out=fsb[0:8, c * 512:c * 512 + n], in_=k.ps[b][0:8, 0:n]),
             reads=[("ps", b)], writes=[("fsb", c)])
    p.dma("sync", k.F_h.ap(), fsb[0:8, :], reads=[("fsb", c) for c in range(3)], writes=["F_d"])
    p.barrier()
    a.release(m)


def stage_attn(k, l, heads):
    p, a = k.p, k.arena
    m0 = a.mark()
    heads = list(heads)
    lam_init = 0.8 - 0.6 * math.exp(-0.3 * l)
    lq = a.f32(256).rearrange("p (a b) -> p a b", a=4)
    for i, n in enumerate(["lambda_q1", "lambda_k1", "lambda_q2", "lambda_k2"]):
        p.dma("sync", lq[:, i, :], k.din[n][l].partition_broadcast(128), writes=[("lam_in", i)])
    lprod = a.f32(128).rearrange("p (a b) -> p a b", a=2)
    lsum = a.f32(2)
    neglam = a.f32(1)
    for i in range(2):
        p.op("vector", lambda E, i=i: E.tensor_tensor(out=lprod[:, i, :], in0=lq[:, 2 * i, :], in1=lq[:, 2 * i + 1, :],
                                                      op=ALU.mult),
             reads=[("lam_in", 2 * i), ("lam_in", 2 * i + 1)], writes=[("lprod", i)])
    p.op("vector", lambda E: E.tensor_reduce(out=lsum, in_=lprod, axis=AX.X, op=ALU.add),
         reads=[("lprod", 0), ("lprod", 1)], writes=["lsum"])
    p.op("scalar", lambda E: E.activation(out=lsum, in_=lsum, func=AF.Exp), reads=["lsum"], writes=["lsum"])
    p.op("vector", lambda E: E.scalar_tensor_tensor(out=neglam, in0=lsum[:, 1:2], scalar=-lam_init, in1=lsum[:, 0:1],
                                                    op0=ALU.add, op1=ALU.subtract), reads=["lsum"], writes=["neglam"])
    gsub = a.f32(128)
    p.dma("sync", gsub, k.din["subln_g"][l].partition_broadcast(128), writes=[("attn", "gain")])
    p.op("vector", lambda E: E.tensor_scalar(out=gsub, in0=gsub, scalar1=1.0 - lam_init, scalar2=None, op0=ALU.mult),
         reads=[("attn", "gain")], writes=[("attn", "gain")])
    bfar = a.f32(16).rearrange("p (a b) -> p a b", a=2)
    p.dma("sync", bfar, bass.AP(k.dh["rel_bias"], 15 * 8, [[0, 128], [128, 2], [1, 8]]), writes=["bfar"])

    self_f = a.f32(256)
    p.dma("sync", self_f, k.din["c_sel"], writes=["sel_f"])
    sel = a.bf16(256)
    p.op("vector", lambda E: E.tensor_copy(out=sel, in_=self_f), reads=["sel_f"], writes=["sel"])
    rball = a.f32(256)
    p.dma("sync", rball, k.din["rel_bias"].rearrange("b h -> (b h)").partition_broadcast(128), writes=["rball"])
    maxb = a.f32(8)
    p.op("vector", lambda E: E.tensor_reduce(out=maxb, in_=rball.rearrange("p (b h) -> p h b", h=8), axis=AX.X,
                                             op=ALU.max), reads=["rball"], writes=["maxb"])
    sqb = a.bf16(2048)
    mx = a.f32(16).rearrange("p (w m t) -> p w m t", w=2, m=2)
    mx2 = a.f32(4).rearrange("p (w m) -> p w m", w=2)
    negM = a.f32(2)
    bfm = a.f32(4).rearrange("p (m s) -> p m s", m=2)
    wts = [a.bf16(16 * 512).rearrange("p (c n) -> p c n", c=16) for _ in range(2)]
    qT = a.bf16(2048)
    kT = a.bf16(2048)
    Vaug = a.bf16(16 * 130).rearrange("p (t c) -> p t c", t=16)
    gA = a.f32(2048).rearrange("p (a b) -> p a b", a=16)
    biasT = [a.f32(256) for _ in range(4)]
    tprime = [a.f32(256) for _ in range(2)]
    PT = [a.bf16(16 * 256).rearrange("p (kb q) -> p kb q", kb=16) for _ in range(2)]
    o1 = a.f32(2048).rearrange("p (a b) -> p a b", a=16)
    oA = a.f32(2048).rearrange("p (a b) -> p a b", a=16)
    tmpS = [a.f32(256) for _ in range(2)]
    rs = a.f32(32)
    p.op("vector", lambda E: E.memset(Vaug[:, :, 128:130], 1.0), writes=["Vones"])

    def wsegs(h):
        return [(h * 128, 128), (1024 + h * 128, 128), (2048 + h * 128, 128), (3072 + h * 128, 128)]

    wkeys_next = load_w(k, wts[0], l, wsegs(heads[0]), ("wA", 0))
    for hi, h in enumerate(heads):
        wt = wts[hi % 2]
        wkeys = wkeys_next
        if hi + 1 < len(heads):
            wkeys_next = load_w(k, wts[(hi + 1) % 2], l, wsegs(heads[hi + 1]), ("wA", (hi + 1) % 2))
        for Di, Dv in enumerate([-1, 0, 1, 2]):
            c = 512 - 128 * Dv
            tp = tprime[Di % 2]
            p.dma("sync", tp, bass.AP(k.F_h, h * 1280 + c, [[1, 128], [1, 256]]), reads=["F_d"],
                  writes=[("tprime", Di % 2)])
            b = k.next_bank()
            p.op("tensor", lambda E, b=b, tp=tp: E.matmul(k.ps[b][:, 0:256], lhsT=k.J, rhs=tp, start=True, stop=True),
                 reads=["J", ("tprime", Di % 2)], writes=[("ps", b)])
            evac_copy(k, Di, biasT[Di], k.ps[b][:, 0:256], [("ps", b)], [("biasT", Di)])
        ev = 0
        for which, dst in ((0, qT), (1, kT)):
            for tq in range(4):
                b = k.next_bank()
                for kc in range(16):
                    p.op("tensor", lambda E, b=b, kc=kc, which=which, tq=tq, wt=wt: E.matmul(
                        k.ps[b], lhsT=wt[:, kc, which * 128:(which + 1) * 128],
                        rhs=k.uT[:, kc, tq * 512:(tq + 1) * 512], start=(kc == 0), stop=(kc == 15)),
                        reads=wkeys + [("uT", kc, tq)], writes=[("ps", b)])
                evac_copy(k, ev, dst[:, tq * 512:(tq + 1) * 512], k.ps[b], [("ps", b)], [("qk", which, tq)])
                ev += 1
        for tt in range(16):
            b = k.next_bank()
            for kc in range(16):
                p.op("tensor", lambda E, b=b, kc=kc, tt=tt, wt=wt: E.matmul(
                    k.ps[b][:, 0:256], lhsT=k.uT[:, kc, tt * 128:(tt + 1) * 128], rhs=wt[:, kc, 256:512],
                    start=(kc == 0), stop=(kc == 15)),
                    reads=wkeys + [("uT", kc, tt // 4)], writes=[("ps", b)])
            p.op("vector", lambda E, b=b, tt=tt: E.tensor_copy(out=Vaug[:, tt, 0:128], in_=k.ps[b][:, 0:128]),
                 reads=[("ps", b)], writes=[("V", tt), ("psr", b)])
            p.op("scalar", lambda E, b=b, tt=tt: E.activation(out=gA[:, tt, :], in_=k.ps[b][:, 128:256], func=AF.Silu),
                 reads=[("ps", b)], writes=[("attn", "gate"), ("psr", b)])
        for wi, src in enumerate((qT, kT)):
            p.op("gpsimd", lambda E, src=src: E.tensor_tensor(out=sqb, in0=src, in1=src, op=ALU.mult),
                 reads=[("qk", wi, t_) for t_ in range(4)], writes=["sqb"])
            for mm in range(2):
                for tq in range(4):
                    b = k.next_bank()
                    p.op("tensor", lambda E, b=b, mm=mm, tq=tq: E.matmul(
                        k.ps[b], lhsT=sel[:, mm * 128:(mm + 1) * 128], rhs=sqb[:, tq * 512:(tq + 1) * 512], start=True,
                        stop=True), reads=["sel", "sqb"], writes=[("ps", b)])
                    p.op("vector", lambda E, b=b, wi=wi, mm=mm, tq=tq: E.tensor_reduce(
                        out=mx[:, wi, mm, tq:tq + 1], in_=k.ps[b], axis=AX.X, op=ALU.max), reads=[("ps", b)],
                        writes=[("mx", wi, mm, tq)])
        p.op("vector", lambda E: E.tensor_reduce(out=mx2, in_=mx, axis=AX.X, op=ALU.max),
             reads=[("mx", w_, m_, t_) for w_ in range(2) for m_ in range(2) for t_ in range(4)], writes=["mx2"])
        p.op("vector", lambda E: E.tensor_tensor(out=negM, in0=mx2[:, 0, :], in1=mx2[:, 1, :], op=ALU.mult),
             reads=["mx2"], writes=["negM"])
        p.op("scalar", lambda E: E.activation(out=negM, in_=negM, func=AF.Sqrt), reads=["negM"], writes=["negM"])
        p.op("vector", lambda E, h=h: E.tensor_scalar(out=negM, in0=negM, scalar1=-0.125 * 1.02,
                                                      scalar2=maxb[:, h:h + 1], op0=ALU.mult, op1=ALU.subtract),
             reads=["negM", "maxb"], writes=["negM"])
        for mm in range(2):
            p.op("vector", lambda E, mm=mm, h=h: E.tensor_scalar(out=bfm[:, mm, :], in0=bfar[:, :, h],
                                                                 scalar1=negM[:, mm:mm + 1], scalar2=None, op0=ALU.add),
                 reads=["negM", "bfar"], writes=[("bfm", mm)])
        for mm in range(2):
            for qt in range(8):
                pi = (mm * 8 + qt) % 2
                PTb = PT[pi]
                for kb in range(16):
                    b = k.next_bank()
                    p.op("tensor", lambda E, b=b, kb=kb, qt=qt, mm=mm: E.matmul(
                        k.ps[b][:, 0:256], lhsT=kT[mm * 64:(mm + 1) * 64, kb * 128:(kb + 1) * 128],
                        rhs=qT[mm * 64:(mm + 1) * 64, qt * 256:(qt + 1) * 256], start=True, stop=True),
                        reads=[("qk", 1, kb // 4), ("qk", 0, qt // 2)], writes=[("ps", b)])
                    Dv = kb - 2 * qt
                    if -1 <= Dv <= 2:
                        ts_ = tmpS[kb % 2]
                        p.op("vector", lambda E, b=b, ts_=ts_, Dv=Dv: E.scalar_tensor_tensor(
                            out=ts_, in0=k.ps[b][:, 0:256], scalar=0.125, in1=biasT[Dv + 1], op0=ALU.mult, op1=ALU.add),
                            reads=[("ps", b), ("biasT", Dv + 1)], writes=[("tmpS", kb % 2)])
                        p.op("scalar", lambda E, ts_=ts_, kb=kb, PTb=PTb, mm=mm: E.activation(
                            out=PTb[:, kb, :], in_=ts_, func=AF.Exp, bias=negM[:, mm:mm + 1]),
                             reads=[("tmpS", kb % 2), "negM"], writes=[("PT", pi, kb)])
                    else:
                        side = 1 if Dv >= 3 else 0
                        p.op("scalar", lambda E, b=b, kb=kb, PTb=PTb, side=side, mm=mm: E.activation(
                            out=PTb[:, kb, :], in_=k.ps[b][:, 0:256], func=AF.Exp, bias=bfm[:, mm, side:side + 1],
                            scale=0.125), reads=[("ps", b), ("bfm", mm)], writes=[("PT", pi, kb)])
                for qs in range(2):
                    qb = qt * 2 + qs
                    b = k.next_bank()
                    for kb in range(16):
                        p.op("tensor", lambda E, b=b, kb=kb, qs=qs, PTb=PTb: E.matmul(
                            k.ps[b][:, 0:130], lhsT=PTb[:, kb, qs * 128:(qs + 1) * 128], rhs=Vaug[:, kb, 0:130],
                            start=(kb == 0), stop=(kb == 15)),
                            reads=[("PT", pi, kb), ("V", kb), "Vones"], writes=[("ps", b)])
                    rsq = rs[:, mm * 16 + qb:mm * 16 + qb + 1]
                    rk = ("rs", mm, qb)
                    p.op("vector", lambda E, b=b, rsq=rsq: E.reciprocal(out=rsq, in_=k.ps[b][:, 128:129]),
                         reads=[("ps", b)], writes=[rk])
                    if mm == 0:
                        p.op("vector", lambda E, b=b, rsq=rsq, qb=qb: E.tensor_scalar(
                            out=o1[:, qb, :], in0=k.ps[b][:, 0:128], scalar1=rsq, scalar2=None, op0=ALU.mult),
                            reads=[("ps", b), rk], writes=[("o1", qb)])
                    else:
                        p.op("vector", lambda E, rsq=rsq: E.tensor_scalar(out=rsq, in0=rsq, scalar1=neglam[:, 0:1],
                                                                          scalar2=None, op0=ALU.mult),
                             reads=[rk, "neglam"], writes=[rk])
                        p.op("vector", lambda E, b=b, rsq=rsq, qb=qb: E.scalar_tensor_tensor(
                            out=oA[:, qb, :], in0=k.ps[b][:, 0:128], scalar=rsq, in1=o1[:, qb, :], op0=ALU.mult,
                            op1=ALU.add), reads=[("ps", b), rk, ("o1", qb)], writes=[("attn", "o")])
        epilogue(k, oA, gsub, gA, k.eps_t[:, 1:2], 1.0 / 128, h * 128, "attn")
    p.barrier()
    a.release(m0)


def stage_out(k, l, xsrc, ydst):
    p, a = k.p, k.arena
    m = a.mark()
    wo = a.bf16(24 * D).rearrange("p (c d) -> p c d", c=24)
    wkeys = []
    for c4 in range(6):
        p.dma("gpsimd", wo[:, c4 * 4:(c4 + 1) * 4, :],
              k.din["w_out"][l][c4 * 512:(c4 + 1) * 512, :].rearrange("(c p) d -> p c d", p=128), writes=[("wo", c4)])
        wkeys.append(("wo", c4))
    a2 = Arena(k.uT_f32, 16 * S // 2)
    gpost = a2.f32(D)
    p.dma("sync", gpost, k.din["post_norm_g"][l].partition_broadcast(128), writes=["gpost"])
    mts = [a2.bf16(24 * 128).rearrange("p (c t) -> p c t", c=24) for _ in range(2)]
    xts = [a2.f32(D) for _ in range(2)]
    ots = [a2.f32(D) for _ in range(2)]
    junk = a2.bf16(D)
    ss = a.f32(16)
    p.op("vector", lambda E: E.memset(ss, 0.0), writes=["o_ss"])
    mrows = [i * 128 for i in range(24)]
    for tt in range(16):
        mt, xt, ot = mts[tt % 2], xts[tt % 2], ots[tt % 2]
        kmt, kxt, kot = ("o_mt", tt % 2), ("o_xt", tt % 2), ("o_ot", tt % 2)
        p.dma("sync", mt, k.mixedT_d[:, tt * 128:(tt + 1) * 128].rearrange("(c p) t -> p c t", p=128),
              reads=[("mixedT", r) for r in mrows], writes=[kmt])
        p.dma("sync", xt, xsrc[tt * 128:(tt + 1) * 128, :], writes=[kxt])
        for n in range(4):
            b = k.next_bank()
            for c in range(24):
                p.op("tensor", lambda E, b=b, c=c, n=n, mt=mt: E.matmul(
                    k.ps[b], lhsT=mt[:, c, :], rhs=wo[:, c, n * 512:(n + 1) * 512], start=(c == 0), stop=(c == 23)),
                    reads=[kmt] + wkeys, writes=[("ps", b)])
            evac_copy(k, n, ot[:, n * 512:(n + 1) * 512], k.ps[b], [("ps", b)], [(kot, n)])
        okeys = [(kot, n) for n in range(4)]
        p.op("scalar", lambda E, ot=ot, tt=tt: E.activation(out=junk, in_=ot, func=AF.Square,
                                                            accum_out=ss[:, tt:tt + 1]),
             reads=okeys + ["o_ss"], writes=["o_junk", ("o_ssv", tt)])
        p.op("scalar", lambda E, tt=tt: E.activation(out=ss[:, tt:tt + 1], in_=ss[:, tt:tt + 1], func=AF.Sqrt,
                                                     bias=k.eps_t[:, 0:1], scale=1.0 / D),
             reads=[("o_ssv", tt), "eps"], writes=[("o_ssv", tt)])
        p.op("vector", lambda E, tt=tt: E.reciprocal(out=ss[:, tt:tt + 1], in_=ss[:, tt:tt + 1]),
             reads=[("o_ssv", tt)], writes=[("o_ssv", tt)])
        p.op("vector", lambda E, ot=ot, tt=tt: E.scalar_tensor_tensor(out=ot, in0=ot, scalar=ss[:, tt:tt + 1], in1=gpost,
                                                                     op0=ALU.mult, op1=ALU.mult),
             reads=okeys + [("o_ssv", tt), "gpost"], writes=okeys)
        p.op("gpsimd", lambda E, ot=ot, xt=xt: E.tensor_tensor(out=ot, in0=ot, in1=xt, op=ALU.add),
             reads=okeys + [kxt], writes=okeys)
        p.dma("sync", ydst[tt * 128:(tt + 1) * 128, :], ot, reads=okeys, writes=[("xout", l, tt)])
    p.barrier()
    a.release(m)


def stage_hgrn(k, l, heads):
    p, a = k.p, k.arena
    m0 = a.mark()
    heads = list(heads)
    lbt = a.f32(16).rearrange("p (d h) -> p d h", d=2)
    oml = a.f32(16).rearrange("p (d h) -> p d h", d=2)
    if l == 0:
        p.op("vector", lambda E: E.memset(lbt, 0.0), writes=["lbt"])
        p.op("vector", lambda E: E.memset(oml, 1.0), writes=["oml"])
    else:
        lg = a.f32(32).rearrange("p (d l h) -> p d l h", d=2, l=2)
        s0 = a.f32(16).rearrange("p (d h) -> p d h", d=2)
        s1 = a.f32(16).rearrange("p (d h) -> p d h", d=2)
        lk = []
        for d in range(2):
            for ll in range(2):
                p.dma("sync", lg[:, d, ll, :], k.din["hgrn_lb_logits"][d, ll].rearrange("(h p) -> p h", p=128),
                      writes=[("lg", d, ll)])
                lk.append(("lg", d, ll))
        p.op("vector", lambda E: E.tensor_tensor(out=s0, in0=lg[:, :, 0, :], in1=lg[:, :, 1, :], op=ALU.subtract),
             reads=lk, writes=["lb_s0"])
        p.op("scalar", lambda E: E.activation(out=s1, in_=s0, func=AF.Sigmoid, scale=-1.0), reads=["lb_s0"],
             writes=["lb_s1"])
        p.op("scalar", lambda E: E.activation(out=s0, in_=s0, func=AF.Sigmoid), reads=["lb_s0", "lb_s1"],
             writes=["lb_s0"])
        p.op("vector", lambda E: E.tensor_tensor(out=s1, in0=s0, in1=s1, op=ALU.add), reads=["lb_s0", "lb_s1"],
             writes=["lb_s1"])
        p.op("vector", lambda E: E.tensor_tensor(out=lbt, in0=s1, in1=s0, op=ALU.subtract), reads=["lb_s0", "lb_s1"],
             writes=["lbt"])
        p.op("vector", lambda E: E.tensor_scalar(out=oml, in0=lbt, scalar1=-1.0, scalar2=1.0, op0=ALU.mult,
                                                 op1=ALU.add), reads=["lbt"], writes=["oml"])
    cm0 = a.f32(S)
    p.op("vector", lambda E: E.memset(cm0, 1.0), writes=["cm0"])
    p.op("vector", lambda E: E.memset(cm0.rearrange("p (c t) -> p c t", t=64)[:, :, 0:1], 0.0), writes=["cm0"])
    gainC = a.f32(128)
    p.dma("sync", gainC, k.din["hgrn_norm_g"][l].partition_broadcast(128), writes=[("hgrn", "gain")])
    masks = [a.f32(128), a.f32(128)]
    p.dma("sync", masks[0], k.din["c_maskF"], writes=["maskF"])
    p.dma("sync", masks[1], k.din["c_maskB"], writes=["maskB"])
    wt = a.bf16(16 * 640).rearrange("p (c n) -> p c n", c=16)
    qT = a.f32(S)
    B1, B2, B3, B4, B5 = [a.f32(S) for _ in range(5)]
    qtl, ktl, qha, qhb = [a.bf16(S) for _ in range(4)]
    khat = a.bf16(S).rearrange("p (t j) -> p t j", t=16)
    V = a.bf16(S).rearrange("p (t j) -> p t j", t=16)
    gC = a.f32(S).rearrange("p (t j) -> p t j", t=16)
    oacc = a.f32(S).rearrange("p (t j) -> p t j", t=16)
    Sf = a.f32(128)
    Sbs = [a.bf16(128), a.bf16(128)]
    scT = a.bf16(128)
    totb = a.f32(32)
    refb = a.f32(32)
    dec = a.f32(32)
    p.op("vector", lambda E: E.memset(qha, 0.0), writes=["qha"])
    p.op("vector", lambda E: E.memset(qhb, 0.0), writes=["qhb"])

    def v3(x):
        return x.rearrange("p (c t) -> p c t", t=64)

    def v4(x):
        return x.rearrange("p (n h t) -> p n h t", h=2, t=64)

    for hi, h in enumerate(heads):
        segs = [(8448 + h * 128, 128), (10496 + h * 128, 128), (11520 + h * 128, 128), (9472 + h * 128, 128),
                (12544 + h * 128, 128)]
        wkeys = load_w(k, wt, l, segs, "wC")
        for tq in range(4):
            b = k.next_bank()
            for kc in range(16):
                p.op("tensor", lambda E, b=b, kc=kc, tq=tq: E.matmul(
                    k.ps[b], lhsT=wt[:, kc, 0:128], rhs=k.uT[:, kc, tq * 512:(tq + 1) * 512], start=(kc == 0),
                    stop=(kc == 15)), reads=wkeys + [("uT", kc, tq)], writes=[("ps", b)])
            evac_copy(k, tq, qT[:, tq * 512:(tq + 1) * 512], k.ps[b], [("ps", b)], ["c_qT"])
        for tt in range(16):
            b = k.next_bank()
            for kc in range(16):
                p.op("tensor", lambda E, b=b, kc=kc, tt=tt: E.matmul(
                    k.ps[b][:, 0:256], lhsT=k.uT[:, kc, tt * 128:(tt + 1) * 128], rhs=wt[:, kc, 384:640],
                    start=(kc == 0), stop=(kc == 15)), reads=wkeys + [("uT", kc, tt // 4)], writes=[("ps", b)])
            p.op("vector", lambda E, b=b, tt=tt: E.tensor_copy(out=V[:, tt, :], in_=k.ps[b][:, 0:128]),
                 reads=[("ps", b)], writes=[("cV", tt), ("psr", b)])
            p.op("scalar", lambda E, b=b, tt=tt: E.activation(out=gC[:, tt, :], in_=k.ps[b][:, 128:256], func=AF.Silu),
                 reads=[("ps", b)], writes=[("hgrn", "gate"), ("psr", b)])
        for d in range(2):
            for tq in range(4):
                b = k.next_bank()
                for kc in range(16):
                    p.op("tensor", lambda E, b=b, kc=kc, tq=tq, d=d: E.matmul(
                        k.ps[b], lhsT=wt[:, kc, 128 + 128 * d:256 + 128 * d], rhs=k.uT[:, kc, tq * 512:(tq + 1) * 512],
                        start=(kc == 0), stop=(kc == 15)), reads=wkeys + [("uT", kc, tq)], writes=[("ps", b)])
                p.op("scalar", lambda E, b=b, tq=tq: E.activation(out=B1[:, tq * 512:(tq + 1) * 512], in_=k.ps[b],
                                                                  func=AF.Sigmoid), reads=[("ps", b)], writes=["B1"])
            p.op("vector", lambda E, d=d, h=h: E.tensor_scalar(out=B1, in0=B1, scalar1=oml[:, d, h:h + 1],
                                                               scalar2=lbt[:, d, h:h + 1], op0=ALU.mult, op1=ALU.add),
                 reads=["B1", "oml", "lbt"], writes=["B1"])
            p.op("scalar", lambda E: E.activation(out=B2, in_=B1, func=AF.Ln), reads=["B1"], writes=["B2"])
            p.op("vector", lambda E: E.tensor_scalar(out=B1, in0=B1, scalar1=-1.0, scalar2=1.0, op0=ALU.mult,
                                                     op1=ALU.add), reads=["B1", "B2"], writes=["B1"])
            p.op("vector", lambda E: E.tensor_tensor_scan(out=B3, data0=cm0, data1=B2, initial=0.0, op0=ALU.mult,
                                                          op1=ALU.add), reads=["cm0", "B2"], writes=["B3"])
            p.op("vector", lambda E: E.tensor_copy(out=totb, in_=v3(B3)[:, :, 63]), reads=["B3"], writes=["totb"])
            if d == 1:
                p.op("vector", lambda E: E.scalar_tensor_tensor(out=B3, in0=B3, scalar=-1.0, in1=B2, op0=ALU.mult,
                                                                op1=ALU.add), reads=["B3", "B2", "totb"], writes=["B3"])
                p.op("vector", lambda E: E.tensor_tensor(out=v3(B3), in0=v3(B3), in1=bc_last(totb, 64), op=ALU.add),
                     reads=["B3", "totb"], writes=["B3"])
            mid = 31 if d == 0 else 32
            p.op("vector", lambda E, mid=mid: E.tensor_copy(out=refb, in_=v3(B3)[:, :, mid]), reads=["B3"],
                 writes=["refb"])
            p.op("scalar", lambda E: E.activation(out=dec, in_=totb, func=AF.Exp), reads=["totb"], writes=["dec"])
            p.op("vector", lambda E: E.tensor_tensor(out=v3(B4), in0=v3(B3), in1=bc_last(refb, 64), op=ALU.subtract),
                 reads=["B3", "refb"], writes=["B4"])
            p.op("scalar", lambda E: E.activation(out=B2, in_=B4, func=AF.Exp), reads=["B4"], writes=["B2"])
            p.op("vector", lambda E: E.tensor_tensor(out=qtl, in0=qT, in1=B2, op=ALU.mult), reads=["c_qT", "B2"],
                 writes=["qtl"])
            p.op("scalar", lambda E: E.activation(out=B2, in_=B4, func=AF.Exp, scale=-1.0), reads=["B4", "qtl"],
                 writes=["B2"])
            p.op("vector", lambda E: E.tensor_tensor(out=ktl, in0=B1, in1=B2, op=ALU.mult), reads=["B1", "B2"],
                 writes=["ktl"])
            p.op("scalar", lambda E: E.activation(out=B2, in_=B3, func=AF.Exp), reads=["B3", "ktl"], writes=["B2"])
            p.op("vector", lambda E: E.tensor_tensor(out=v4(qha)[:, :, 0, :], in0=v4(qT)[:, :, 0, :],
                                                     in1=v4(B2)[:, :, 0, :], op=ALU.mult), reads=["c_qT", "B2"],
                 writes=["qha"])
            p.op("vector", lambda E: E.tensor_tensor(out=v4(qhb)[:, :, 1, :], in0=v4(qT)[:, :, 1, :],
                                                     in1=v4(B2)[:, :, 1, :], op=ALU.mult), reads=["c_qT", "B2"],
                 writes=["qhb"])
            p.op("vector", lambda E: E.scalar_tensor_tensor(out=v3(B4), in0=v3(B3), scalar=-1.0, in1=bc_last(totb, 64),
                                                            op0=ALU.mult, op1=ALU.add), reads=["B3", "totb", "B2"],
                 writes=["B4"])
            p.op("scalar", lambda E: E.activation(out=B2, in_=B4, func=AF.Exp), reads=["B4", "qha", "qhb"],
                 writes=["B2"])
            p.op("vector", lambda E: E.tensor_tensor(out=B5, in0=B1, in1=B2, op=ALU.mult), reads=["B1", "B2"],
                 writes=["B5"])
            for g4 in range(4):
                b = k.next_bank()
                for i in range(4):
                    tt = g4 * 4 + i
                    p.op("tensor", lambda E, b=b, i=i, tt=tt: E.transpose(
                        out=k.ps[b][:, i * 128:(i + 1) * 128], in_=B5[:, tt * 128:(tt + 1) * 128], identity=k.ident_f),
                        reads=["B5", "ident_f"], writes=[("ps", b)])
                evac_copy(k, g4, khat[:, g4 * 4:(g4 + 1) * 4, :],
                          k.ps[b].rearrange("p (t j) -> p t j", t=4), [("ps", b)], [("khat", g4)])
            p.op("vector", lambda E: E.memset(Sf, 0.0), writes=["Sf"])
            p.op("vector", lambda E: E.memset(Sbs[0], 0.0), writes=[("Sb", 0)])
            si = 0
            tts = range(16) if d == 0 else range(15, -1, -1)
            halves = (0, 1) if d == 0 else (1, 0)
            qhs = (qha, qhb)
            for tt in tts:
                cs = slice(tt * 128, (tt + 1) * 128)
                bS = k.next_bank()
                p.op("tensor", lambda E, bS=bS, cs=cs: E.matmul(k.ps[bS][:, 0:128], lhsT=ktl[:, cs], rhs=qtl[:, cs],
                                                                start=True, stop=True),
                     reads=["ktl", "qtl"], writes=[("ps", bS)])
                p.op("vector", lambda E, bS=bS, d=d: E.tensor_tensor(out=scT, in0=k.ps[bS][:, 0:128], in1=masks[d],
                                                                     op=ALU.mult),
                     reads=[("ps", bS), "maskF", "maskB"], writes=["scT"])
                bO = k.next_bank()
                p.op("tensor", lambda E, bO=bO, tt=tt: E.matmul(k.ps[bO][:, 0:128], lhsT=scT, rhs=V[:, tt, :],
                                                                start=True, stop=False),
                     reads=["scT", ("cV", tt)], writes=[("ps", bO)])
                for hn, hf in enumerate(halves):
                    p.op("tensor", lambda E, bO=bO, cs=cs, hf=hf, si=si, hn=hn: E.matmul(
                        k.ps[bO][:, 0:128], lhsT=qhs[hf][:, cs], rhs=Sbs[si], start=False, stop=(hn == 1)),
                        reads=["qha", "qhb", ("Sb", si)], writes=[("ps", bO)])
                    bU = k.next_bank()
                    ps_ = slice(hf * 64, (hf + 1) * 64)
                    p.op("tensor", lambda E, bU=bU, ps_=ps_, tt=tt: E.matmul(
                        k.ps[bU][:, 0:128], lhsT=khat[ps_, tt, :], rhs=V[ps_, tt, :], start=True, stop=True),
                        reads=[("khat", tt // 4), ("cV", tt)], writes=[("ps", bU)])
                    ch = tt * 2 + hf
                    p.op("vector", lambda E, bU=bU, ch=ch: E.scalar_tensor_tensor(
                        out=Sf, in0=Sf, scalar=dec[:, ch:ch + 1], in1=k.ps[bU][:, 0:128], op0=ALU.mult, op1=ALU.add),
                        reads=["Sf", "dec", ("ps", bU)], writes=["Sf"])
                    si ^= 1
                    p.op("scalar", lambda E, si=si: E.activation(out=Sbs[si], in_=Sf, func=AF.Copy), reads=["Sf"],
                         writes=[("Sb", si)])
                if d == 0:
                    p.op("scalar", lambda E, bO=bO, tt=tt: E.activation(out=oacc[:, tt, :], in_=k.ps[bO][:, 0:128],
                                                                        func=AF.Copy),
                         reads=[("ps", bO)], writes=[("oacc", tt)])
                else:
                    p.op("vector", lambda E, bO=bO, tt=tt: E.tensor_tensor(out=oacc[:, tt, :], in0=oacc[:, tt, :],
                                                                           in1=k.ps[bO][:, 0:128], op=ALU.add),
                         reads=[("ps", bO), ("oacc", tt)], writes=[("oacc", tt), ("hgrn", "o")])
        epilogue(k, oacc, gainC, gC, k.eps_t[:, 0:1], 1.0 / 128, 2048 + h * 128, "hgrn")
    p.barrier()
    a.release(m0)


def pslice(buf, p0, n, step):
    ap = [list(x) for x in buf.ap]
    ps = ap[0][0]
    return bass.AP(buf.tensor, buf.offset + p0 * ps, [[ps * step, n]] + ap[1:])


def stage_rwkv(k, l):
    p, a = k.p, k.arena
    a2 = Arena(k.uT_f32, 16 * S // 2)
    rw = {n: t.ap() for n, t in k.rw.items()}
    TT = 16
    m0 = a.mark()
    wts = [a.bf16(16 * 512).rearrange("p (c n) -> p c n", c=16) for _ in range(2)]
    stg = [a.f32(512) for _ in range(3)]
    cts = [(4096 + ct * 512, 512 if ct < 8 else 256) for ct in range(9)]
    wk_next = load_w(k, wts[0], l, [cts[0]], ("wB", 0))
    si = 0
    for ct, (c0, n) in enumerate(cts):
        wt = wts[ct % 2]
        wkeys = wk_next
        if ct + 1 < 9:
            wk_next = load_w(k, wts[(ct + 1) % 2], l, [cts[ct + 1]], ("wB", (ct + 1) % 2))
        for tt in range(TT):
            b = k.next_bank()
            for kc in range(16):
                p.op("tensor", lambda E, b=b, kc=kc, tt=tt, wt=wt, n=n: E.matmul(
                    k.ps[b][:, 0:n], lhsT=k.uT[:, kc, tt * 128:(tt + 1) * 128], rhs=wt[:, kc, 0:n], start=(kc == 0),
                    stop=(kc == 15)), reads=wkeys + [("uT", kc, tt // 4)], writes=[("ps", b)])
            sb = stg[si % 3]
            evac_copy(k, si, sb[:, 0:n], k.ps[b][:, 0:n], [("ps", b)], [("stgB", si % 3)])
            p.dma("sync", k.pB_d[tt * 128:(tt + 1) * 128, ct * 512:ct * 512 + n], sb[:, 0:n], reads=[("stgB", si % 3)],
                  writes=[("pB", tt, ct)])
            si += 1
    p.barrier()
    a.release(m0)
    m0 = a.mark()
    mu0b, mu1b = a2.f32(3328), a2.f32(3328)
    w0b = a2.f32(2048).rearrange("p (d c) -> p d c", d=2)
    a0b = a2.f32(2048).rearrange("p (d c) -> p d c", d=2)
    k_kb, k_ab, r_kb = a2.f32(1024), a2.f32(1024), a2.f32(1024)
    wup, aup = a2.bf16(1024), a2.bf16(1024)
    prm = ["rwp%d" % i for i in range(9)]
    p.dma("sync", mu0b, k.din["rwkv_shift_mu"][l, 0].partition_broadcast(128), writes=[prm[0]])
    p.dma("sync", mu1b, k.din["rwkv_shift_mu"][l, 1].partition_broadcast(128), writes=[prm[1]])
    p.dma("sync", w0b.rearrange("p d c -> p (d c)"),
          k.din["rwkv_w0"][l].rearrange("d c -> (d c)").partition_broadcast(128), writes=[prm[2]])
    p.dma("sync", a0b.rearrange("p d c -> p (d c)"),
          k.din["rwkv_a0"][l].rearrange("d c -> (d c)").partition_broadcast(128), writes=[prm[3]])
    p.dma("sync", k_kb, k.din["rwkv_k_k"][l].partition_broadcast(128), writes=[prm[4]])
    p.dma("sync", k_ab, k.din["rwkv_k_a"][l].partition_broadcast(128), writes=[prm[5]])
    p.dma("sync", r_kb, k.din["rwkv_r_k"][l].partition_broadcast(128), writes=[prm[6]])
    p.dma("gpsimd", wup, k.din["rwkv_w_up"][l].rearrange("d l c -> (d l) c"), writes=[prm[7]])
    p.dma("gpsimd", aup, k.din["rwkv_a_up"][l].rearrange("d l c -> (d l) c"), writes=[prm[8]])
    P0, Pp, Pn = a.f32(3328), a.f32(3328), a.f32(3328)
    KKt, RK, tmp = a.f32(1024), a.f32(1024), a.f32(1024)
    Wt, At, KAt, KMt = a.f32(1024), a.f32(1024), a.f32(1024), a.f32(1024)
    twb = a.bf16(256)
    twT = a.bf16(256)
    n2 = a.f32(16)
    bon = a.f32(32).rearrange("p (d h) -> p d h", d=2)
    bsum = a.f32(16)

    def g3(x):
        return x.rearrange("p (h j) -> p h j", j=64)

    allct = list(range(9))
    for tt in range(TT):
        r0 = tt * 128
        rd = [("pB", t_, c_) for t_ in (tt - 1, tt, tt + 1) if 0 <= t_ < TT for c_ in range(7)]
        p.dma("sync", P0, k.pB_d[r0:r0 + 128, 0:3328], reads=rd, writes=["P0"])
        if tt == 0:
            p.op("vector", lambda E: E.memset(Pp, 0.0), writes=["Pp"])
        else:
            p.dma("sync", Pp[0:1, :], k.pB_d[r0 - 1:r0, 0:3328], reads=rd, writes=["Pp0"])
        p.dma("sync", Pp[1:128, :], k.pB_d[r0:r0 + 127, 0:3328], reads=rd + ["Pp"], writes=["Pp1"])
        if tt == TT - 1:
            p.op("vector", lambda E: E.memset(Pn, 0.0), reads=["Pn0", "Pn1"], writes=["Pn"])
        else:
            p.dma("sync", Pn[127:128, :], k.pB_d[r0 + 128:r0 + 129, 0:3328], reads=rd, writes=["Pn0"])
        p.dma("sync", Pn[0:127, :], k.pB_d[r0 + 1:r0 + 128, 0:3328], reads=rd + ["Pn"], writes=["Pn1"])
        PpK, PnK = ["Pp", "Pp0", "Pp1"], ["Pn", "Pn0", "Pn1"]
        p.op("gpsimd", lambda E: E.tensor_tensor(out=Pp, in0=Pp, in1=P0, op=ALU.subtract), reads=PpK + ["P0"], writes=PpK)
        p.op("vector", lambda E: E.tensor_tensor(out=Pn, in0=Pn, in1=P0, op=ALU.subtract), reads=PnK + ["P0"], writes=PnK)
        p.op("gpsimd", lambda E: E.tensor_tensor(out=Pp, in0=Pp, in1=mu0b, op=ALU.mult), reads=PpK + [prm[0]], writes=PpK)
        p.op("vector", lambda E: E.tensor_tensor(out=Pn, in0=Pn, in1=mu1b, op=ALU.mult), reads=PnK + [prm[1]], writes=PnK)
        p.op("vector", lambda E: E.tensor_tensor(out=P0, in0=P0, in1=Pp, op=ALU.add), reads=PpK + ["P0"], writes=["P0"])
        p.op("vector", lambda E: E.tensor_tensor(out=P0, in0=P0, in1=Pn, op=ALU.add), reads=PnK + ["P0"], writes=["P0"])
        r_, kx, v_ = P0[:, 0:1024], P0[:, 1024:2048], P0[:, 2048:3072]
        p.dma("sync", rw["R"][r0:r0 + 128, :], r_, reads=["P0"], writes=[("rwR", tt)])
        p.dma("sync", rw["V"][r0:r0 + 128, :], v_, reads=["P0"], writes=[("rwV", tt)])
        p.op("scalar", lambda E: E.activation(out=twb[:, 0:128], in_=P0[:, 3072:3200], func=AF.Tanh), reads=["P0"],
             writes=["twb"])
        p.op("scalar", lambda E: E.activation(out=twb[:, 128:256], in_=P0[:, 3200:3328], func=AF.Copy), reads=["P0"],
             writes=["twb"])
        b = k.next_bank()
        psb = k.ps[b].bitcast(BF16)
        for i in range(2):
            p.op("tensor", lambda E, i=i, psb=psb: E.transpose(out=psb[:, i * 128:(i + 1) * 128],
                                                                in_=twb[:, i * 128:(i + 1) * 128], identity=k.ident_bf),
                 reads=["twb", "ident"], writes=[("ps", b)])
        p.op("vector", lambda E, psb=psb: E.tensor_copy(out=twT, in_=psb[:, 0:256]), reads=[("ps", b)], writes=["twT"])
        p.op("vector", lambda E: E.tensor_tensor(out=KKt, in0=kx, in1=k_kb, op=ALU.mult), reads=["P0", prm[4]],
             writes=["KKt"])
        p.op("gpsimd", lambda E: E.tensor_tensor(out=tmp, in0=KKt, in1=KKt, op=ALU.mult), reads=["KKt"], writes=["tmp"])
        p.op("vector", lambda E: E.tensor_reduce(out=n2, in_=g3(tmp), axis=AX.X, op=ALU.add), reads=["tmp"],
             writes=["n2"])
        p.op("scalar", lambda E: E.activation(out=n2, in_=n2, func=AF.Sqrt), reads=["n2"], writes=["n2"])
        p.op("vector", lambda E: E.tensor_scalar(out=n2, in0=n2, scalar1=1e-12, scalar2=None, op0=ALU.max),
             reads=["n2"], writes=["n2"])
        p.op("vector", lambda E: E.reciprocal(out=n2, in_=n2), reads=["n2"], writes=["n2"])
        p.op("vector", lambda E: E.tensor_tensor(out=g3(KKt), in0=g3(KKt), in1=bc_last(n2, 64), op=ALU.mult),
             reads=["KKt", "n2"], writes=["KKt"])
        p.dma("sync", rw["KK"][r0:r0 + 128, :], KKt, reads=["KKt"], writes=[("rwKK", tt)])
        p.op("gpsimd", lambda E: E.tensor_tensor(out=RK, in0=r_, in1=r_kb, op=ALU.mult), reads=["P0", prm[6]],
             writes=["RK"])
        for d in range(2):
            dsl = slice(d * 64, (d + 1) * 64)
            for (dstT, lhs_cols, upw, biasb, pk) in ((Wt, slice(0, 128), wup, w0b, prm[7]),
                                                      (At, slice(128, 256), aup, a0b, prm[8])):
                for hf in range(2):
                    b = k.next_bank()
                    cs = slice(hf * 512, (hf + 1) * 512)
                    p.op("tensor", lambda E, b=b, dsl=dsl, lhs_cols=lhs_cols, upw=upw, cs=cs: E.matmul(
                        k.ps[b], lhsT=twT[dsl, lhs_cols], rhs=upw[dsl, cs], start=True, stop=True),
                        reads=["twT", pk], writes=[("ps", b)])
                    dk = "Wt" if dstT is Wt else "At"
                    p.op("vector", lambda E, b=b, dstT=dstT, cs=cs, biasb=biasb, d=d: E.tensor_tensor(
                        out=dstT[:, cs], in0=k.ps[b], in1=biasb[:, d, cs], op=ALU.add),
                        reads=[("ps", b), prm[2], prm[3]], writes=[dk])
            p.op("scalar", lambda E: E.activation(out=Wt, in_=Wt, func=AF.Sigmoid), reads=["Wt"], writes=["Wt"])
            p.op("scalar", lambda E: E.activation(out=Wt, in_=Wt, func=AF.Exp, scale=-math.exp(-0.5)), reads=["Wt"],
                 writes=["Wt"])
            p.dma("sync", rw["W%d" % d][r0:r0 + 128, :], Wt, reads=["Wt"], writes=[("rwW", d, tt)])
            p.op("scalar", lambda E: E.activation(out=At, in_=At, func=AF.Sigmoid), reads=["At"], writes=["At"])
            p.op("vector", lambda E: E.tensor_tensor(out=KAt, in0=KKt, in1=At, op=ALU.mult), reads=["KKt", "At"],
                 writes=["KAt"])
            p.dma("sync", rw["KA%d" % d][r0:r0 + 128, :], KAt, reads=["KAt"], writes=[("rwKA", d, tt)])
            p.op("vector", lambda E: E.scalar_tensor_tensor(out=At, in0=At, scalar=-1.0, in1=k_ab, op0=ALU.add,
                                                            op1=ALU.mult), reads=["At", prm[5]], writes=["At"])
            p.op("vector", lambda E: E.scalar_tensor_tensor(out=KMt, in0=At, scalar=1.0, in1=kx, op0=ALU.add,
                                                            op1=ALU.mult), reads=["At", "P0"], writes=["KMt"])
            p.dma("sync", rw["KM%d" % d][r0:r0 + 128, :], KMt, reads=["KMt"], writes=[("rwKM", d, tt)])
            p.op("gpsimd", lambda E: E.tensor_tensor(out=tmp, in0=RK, in1=KMt, op=ALU.mult), reads=["RK", "KMt"],
                 writes=["tmp"])
            p.op("vector", lambda E, d=d: E.tensor_reduce(out=bon[:, d, :], in_=g3(tmp), axis=AX.X, op=ALU.add),
                 reads=["tmp"], writes=[("bon", d)])
        p.op("vector", lambda E: E.tensor_tensor(out=bsum, in0=bon[:, 0, :], in1=bon[:, 1, :], op=ALU.add),
             reads=[("bon", 0), ("bon", 1)], writes=["bsum"])
        p.dma("sync", k.bon_d[r0:r0 + 128, :], bsum, reads=["bsum"], writes=[("bon_d", tt)])
    p.barrier()
    a.release(m0)
    m0 = a.mark()
    a2 = Arena(k.uT_f32, 16 * S // 2)
    TC = 32
    NCH = S // TC
    names = ["KK", "W", "KA", "KM", "R"]
    bufs = [{n: a.f32(TC * 64).rearrange("p (t j) -> p t j", j=64) for n in names} for _ in range(2)]
    vb = [a.f32(TC * 16).rearrange("p (t i) -> p t i", i=16) for _ in range(2)]
    yb = [a.f32(TC * 16).rearrange("p (t i) -> p t i", i=16) for _ in range(2)]
    St = a2.f32(1024)
    Sw = a2.f32(1024)
    t1s = [a2.f32(1024), a2.f32(1024)]
    vks = [a2.f32(1024), a2.f32(1024)]
    WRc = [a.f32(TC * 64).rearrange("p (t j) -> p t j", j=64) for _ in range(2)]
    prod = a.f32(TC * 64).rearrange("p (t j) -> p t j", j=64)
    c12 = [a.f32(2 * TC).rearrange("p (w t) -> p w t", w=2) for _ in range(2)]
    SA = [a.f32(TC * 16).rearrange("p (t i) -> p t i", i=16) for _ in range(2)]
    ytmp = a.f32(TC * 16).rearrange("p (t i) -> p t i", i=16)

    def s3(x):
        return x.rearrange("p (i j) -> p i j", j=64)

    p.op("vector", lambda E: E.memset(St, 0.0), writes=["St"])
    allpre = ([("rwR", t_) for t_ in range(TT)] + [("rwV", t_) for t_ in range(TT)] + [("rwKK", t_) for t_ in range(TT)]
              + [(n_, d_, t_) for n_ in ("rwW", "rwKA", "rwKM") for d_ in range(2) for t_ in range(TT)])
    qstate = [0]

    def issue_loads(c):
        bi = c % 2
        B = bufs[bi]
        t0 = c * TC
        for d in range(2):
            tstart = t0 if d == 0 else S - 1 - t0
            tstep = 1024 if d == 0 else -1024
            for n in names:
                src_t = k.rw[n] if n in ("KK", "R") else k.rw["%s%d" % (n, d)]
                for ig in range(4):
                    src = bass.AP(src_t, tstart * 1024, [[64, 16], [tstep, TC], [1, 64]])
                    dst = pslice(B[n], d * 64 + ig, 16, 4)
                    p.dma("sync", dst, src, reads=allpre if c == 0 else [],
                          writes=[("sc", bi, n, d, ig)])
                    qstate[0] += 1
            srcv = bass.AP(k.rw["V"], tstart * 1024, [[16, 64], [tstep, TC], [1, 16]])
            p.dma("sync", vb[bi][d * 64:(d + 1) * 64], srcv,
                  reads=allpre if c == 0 else [], writes=[("scv", bi, d)])
            qstate[0] += 1

    NCH_RUN = k.dbg.get('scan_chunks', NCH)
    issue_loads(0)
    for c in range(NCH_RUN):
        bi = c % 2
        B = bufs[bi]
        t0 = c * TC
        if c + 1 < NCH_RUN:
            issue_loads(c + 1)
        rk_ = {n: [("sc", bi, n, d, ig) for d in range(2) for ig in range(4)] for n in names}
        vkeys = [("scv", bi, 0), ("scv", bi, 1)]
        p.op("vector", lambda E, B=B, bi=bi: E.tensor_tensor(out=WRc[bi], in0=B["W"], in1=B["R"], op=ALU.mult),
             reads=rk_["W"] + rk_["R"], writes=[("WRc", bi)])
        for wi, nm in enumerate(("KM", "KA")):
            p.op("vector", lambda E, B=B, nm=nm: E.tensor_tensor(out=prod, in0=B[nm], in1=B["R"], op=ALU.mult),
                 reads=rk_[nm] + rk_["R"], writes=["prod"])
            p.op("vector", lambda E, bi=bi, wi=wi: E.tensor_reduce(out=c12[bi][:, wi, :], in_=prod, axis=AX.X,
                                                                    op=ALU.add), reads=["prod"], writes=[("c12", bi, wi)])
        p.op("gpsimd", lambda E, B=B, bi=bi: E.tensor_tensor(out=s3(vks[0]), in0=bc_last(vb[bi][:, 0, :], 64),
                                                              in1=bc_mid(B["KM"][:, 0, :], 16), op=ALU.mult),
             reads=vkeys + rk_["KM"], writes=[("vk", 0)])
        for tau in range(TC):
            kkb = bc_mid(B["KK"][:, tau, :], 16)
            wb = bc_mid(B["W"][:, tau, :], 16)
            kab = bc_mid(B["KA"][:, tau, :], 16)
            wrb = bc_mid(WRc[bi][:, tau, :], 16)
            vi = tau % 2
            ta, tb = t1s
            p.op("vector", lambda E, kkb=kkb, ta=ta: E.tensor_tensor(out=s3(ta), in0=s3(St), in1=kkb, op=ALU.mult),
                 reads=["St"] + rk_["KK"], writes=["t1a"])
            p.op("gpsimd", lambda E, wb=wb: E.tensor_tensor(out=s3(Sw), in0=s3(St), in1=wb, op=ALU.mult),
                 reads=["St"] + rk_["W"], writes=["Sw"])
            p.op("vector", lambda E, ta=ta, tau=tau, bi=bi: E.tensor_reduce(out=SA[bi][:, tau, :], in_=s3(ta), axis=AX.X,
                                                                            op=ALU.add), reads=["t1a"],
                 writes=[("SA", bi, tau)])
            p.op("gpsimd", lambda E, vi=vi: E.tensor_tensor(out=Sw, in0=Sw, in1=vks[vi], op=ALU.add),
                 reads=["Sw", ("vk", vi)], writes=["Sw"])
            p.op("vector", lambda E, wrb=wrb, tb=tb: E.tensor_tensor(out=s3(tb), in0=s3(St), in1=wrb, op=ALU.mult),
                 reads=["St", ("WRc", bi)], writes=["t1b"])
            if tau + 1 < TC:
                p.op("gpsimd", lambda E, B=B, bi=bi, tau=tau, vi=vi: E.tensor_tensor(
                    out=s3(vks[1 - vi]), in0=bc_last(vb[bi][:, tau + 1, :], 64), in1=bc_mid(B["KM"][:, tau + 1, :], 16),
                    op=ALU.mult), reads=vkeys + rk_["KM"], writes=[("vk", 1 - vi)])
            p.op("vector", lambda E, tb=tb, tau=tau, bi=bi: E.tensor_reduce(out=yb[bi][:, tau, :], in_=s3(tb), axis=AX.X,
                                                                            op=ALU.add), reads=["t1b"],
                 writes=[("yb", bi)])
            p.op("vector", lambda E, kab=kab, ta=ta, tau=tau, bi=bi: E.tensor_tensor(
                out=s3(ta), in0=bc_last(SA[bi][:, tau, :], 64), in1=kab, op=ALU.mult),
                reads=[("SA", bi, tau)] + rk_["KA"], writes=["t1a"])
            p.op("vector", lambda E, ta=ta: E.tensor_tensor(out=St, in0=Sw, in1=ta, op=ALU.subtract),
                 reads=["Sw", "t1a"], writes=["St"])
        sak = [("SA", bi, t_) for t_ in range(TC)]
        p.op("vector", lambda E, bi=bi: E.tensor_tensor(out=ytmp, in0=vb[bi], in1=bc_last(c12[bi][:, 0, :], 16),
                                                        op=ALU.mult), reads=vkeys + [("c12", bi, 0)], writes=["ytmp"])
        p.op("vector", lambda E, bi=bi: E.tensor_tensor(out=yb[bi], in0=yb[bi], in1=ytmp, op=ALU.add),
             reads=[("yb", bi), "ytmp"], writes=[("yb", bi)])
        p.op("vector", lambda E, bi=bi: E.tensor_tensor(out=ytmp, in0=SA[bi], in1=bc_last(c12[bi][:, 1, :], 16),
                                                        op=ALU.mult), reads=sak + [("c12", bi, 1), ("yb", bi)],
             writes=["ytmp"])
        p.op("vector", lambda E, bi=bi: E.tensor_tensor(out=yb[bi], in0=yb[bi], in1=ytmp, op=ALU.subtract),
             reads=[("yb", bi), "ytmp"], writes=[("yb", bi)])
        for d in range(2):
            tstart = t0 if d == 0 else S - 1 - t0
            tstep = 1024 if d == 0 else -1024
            dsty = bass.AP(k.rw["Y%d" % d], tstart * 1024, [[16, 64], [tstep, TC], [1, 16]])
            p.dma("sync", dsty, yb[bi][d * 64:(d + 1) * 64], reads=[("yb", bi)], writes=[("rwY", d, c)])
    p.barrier()
    a.release(m0)
    m0 = a.mark()
    a2 = Arena(k.uT_f32, 16 * S // 2)
    lgb, lbb = a2.f32(1024), a2.f32(1024)
    p.dma("sync", lgb, k.din["rwkv_lnx_g"][l].partition_broadcast(128), writes=["lgb"])
    p.dma("sync", lbb, k.din["rwkv_lnx_b"][l].partition_broadcast(128), writes=["lbb"])
    Y0t, Y1t, vt, bgt, sqt = a.f32(1024), a.f32(1024), a.f32(1024), a.f32(1024), a.f32(1024)
    bont = a.f32(16)
    mean = a.f32(16)
    var = a.f32(16)
    obt = a.bf16(1024)
    mixTt = a.bf16(1024).rearrange("p (c t) -> p c t", c=8)
    ally = [("rwY", d, c) for d in range(2) for c in range(NCH_RUN)]
    for tt in range(TT):
        r0 = tt * 128
        p.dma("sync", Y0t, rw["Y0"][r0:r0 + 128, :], reads=ally if tt == 0 else [], writes=["Y0t"])
        p.dma("sync", Y1t, rw["Y1"][r0:r0 + 128, :], reads=ally if tt == 0 else [], writes=["Y1t"])
        p.dma("sync", vt, rw["V"][r0:r0 + 128, :], writes=["vt"])
        p.dma("sync", bgt, k.pB_d[r0:r0 + 128, 3328:4352], writes=["bgt"])
        p.dma("sync", bont, k.bon_d[r0:r0 + 128, :], writes=["bont"])
        p.op("vector", lambda E: E.tensor_tensor(out=Y0t, in0=Y0t, in1=Y1t, op=ALU.add), reads=["Y0t", "Y1t"],
             writes=["Y0t"])
        p.op("vector", lambda E: E.tensor_reduce(out=mean, in_=g3(Y0t), axis=AX.X, op=ALU.add), reads=["Y0t"],
             writes=["mean"])
        p.op("vector", lambda E: E.tensor_scalar(out=mean, in0=mean, scalar1=1.0 / 64, scalar2=None, op0=ALU.mult),
             reads=["mean"], writes=["mean"])
        p.op("vector", lambda E: E.tensor_tensor(out=g3(Y0t), in0=g3(Y0t), in1=bc_last(mean, 64), op=ALU.subtract),
             reads=["Y0t", "mean"], writes=["Y0t"])
        p.op("gpsimd", lambda E: E.tensor_tensor(out=sqt, in0=Y0t, in1=Y0t, op=ALU.mult), reads=["Y0t"], writes=["sqt"])
        p.op("vector", lambda E: E.tensor_reduce(out=var, in_=g3(sqt), axis=AX.X, op=ALU.add), reads=["sqt"],
             writes=["var"])
        p.op("scalar", lambda E: E.activation(out=var, in_=var, func=AF.Sqrt, bias=k.eps_t[:, 2:3], scale=1.0 / 64),
             reads=["var", "eps"], writes=["var"])
        p.op("vector", lambda E: E.reciprocal(out=var, in_=var), reads=["var"], writes=["var"])
        p.op("vector", lambda E: E.tensor_tensor(out=g3(Y0t), in0=g3(Y0t), in1=bc_last(var, 64), op=ALU.mult),
             reads=["Y0t", "var"], writes=["Y0t"])
        p.op("gpsimd", lambda E: E.tensor_tensor(out=Y0t, in0=Y0t, in1=lgb, op=ALU.mult), reads=["Y0t", "lgb"],
             writes=["Y0t"])
        p.op("vector", lambda E: E.tensor_tensor(out=Y0t, in0=Y0t, in1=lbb, op=ALU.add), reads=["Y0t", "lbb"],
             writes=["Y0t"])
        p.op("gpsimd", lambda E: E.tensor_tensor(out=g3(vt), in0=g3(vt), in1=bc_last(bont, 64), op=ALU.mult),
             reads=["vt", "bont"], writes=["vt"])
        p.op("vector", lambda E: E.tensor_tensor(out=Y0t, in0=Y0t, in1=vt, op=ALU.add), reads=["Y0t", "vt"],
             writes=["Y0t"])
        p.op("scalar", lambda E: E.activation(out=bgt, in_=bgt, func=AF.Silu), reads=["bgt"], writes=["bgt"])
        p.op("vector", lambda E: E.tensor_tensor(out=obt, in0=Y0t, in1=bgt, op=ALU.mult), reads=["Y0t", "bgt"],
             writes=["obt"])
        b = k.next_bank()
        psb = k.ps[b].bitcast(BF16)
        for cc in range(8):
            p.op("tensor", lambda E, cc=cc, psb=psb: E.transpose(out=psb[:, cc * 128:(cc + 1) * 128],
                                                                  in_=obt[:, cc * 128:(cc + 1) * 128],
                                                                  identity=k.ident_bf),
                 reads=["obt", "ident"], writes=[("ps", b)])
        p.op("vector", lambda E, psb=psb: E.tensor_copy(out=mixTt.rearrange("p c t -> p (c t)"), in_=psb[:, 0:1024]),
             reads=[("ps", b)], writes=["mixTt"])
        p.dma("sync", k.mixedT_d[1024:2048, r0:r0 + 128].rearrange("(c p) t -> p c t", p=128), mixTt, reads=["mixTt"],
              writes=[("mixedT", 1024 + cc * 128) for cc in range(8)])
    p.barrier()
    a.release(m0)


def build(dbg=None):
    dbg = dbg or {}
    nc = bass.Bass("TRN2", target_bir_lowering=False)
    k = K()
    k.nc = nc
    k.dbg = dbg
    din = {}
    dh = {}

    def inp(name, shape, dt=F32):
        dh[name] = nc.dram_tensor(name, list(shape), dt, kind="ExternalInput")
        din[name] = dh[name].ap()

    inp("x", [S, D])
    inp("rel_bias", [32, 8])
    inp("pre_norm_g", [NL, D])
    inp("post_norm_g", [NL, D])
    inp("w_in", [NL, D, PW])
    inp("w_out", [NL, MW, D])
    for n in ["lambda_q1", "lambda_k1", "lambda_q2", "lambda_k2"]:
        inp(n, [NL, 64])
    inp("subln_g", [NL, 128])
    inp("rwkv_shift_mu", [NL, 2, 3328])
    inp("rwkv_w0", [NL, 2, 1024])
    inp("rwkv_w_up", [NL, 2, 64, 1024])
    inp("rwkv_a0", [NL, 2, 1024])
    inp("rwkv_a_up", [NL, 2, 64, 1024])
    for n in ["rwkv_k_k", "rwkv_k_a", "rwkv_r_k", "rwkv_lnx_g", "rwkv_lnx_b"]:
        inp(n, [NL, 1024])
    inp("hgrn_lb_logits", [2, NL, 1024])
    inp("hgrn_norm_g", [NL, 128])
    inp("c_ident", [128, 128])
    inp("c_J", [128, 128])
    inp("c_oh", [32, 1280])
    inp("c_maskF", [128, 128])
    inp("c_maskB", [128, 128])
    inp("c_cm", [128, 3 * S])
    inp("c_sel", [128, 256])
    k.din, k.dh = din, dh
    y = nc.dram_tensor("y", [S, D], F32, kind="ExternalOutput").ap()
    k.y = y
    dout = {}
    for name, (shape, dt) in dbg.get("outs", {}).items():
        dout[name] = nc.dram_tensor(name, list(shape), dt, kind="ExternalOutput").ap()
    k.dout = dout
    k.mixedT_h = nc.dram_tensor("mixedT_d", [MW, S], BF16)
    k.mixedT_d = k.mixedT_h.ap()
    k.x1_d = nc.dram_tensor("x1_d", [S, D], F32).ap()
    k.F_h = nc.dram_tensor("F_d", [8, 1280], F32)
    k.pB_d = nc.dram_tensor("pB_d", [S, 4352], F32).ap()
    k.rw = {}
    for n in ["R", "V", "KK", "W0", "W1", "KA0", "KA1", "KM0", "KM1", "Y0", "Y1"]:
        k.rw[n] = nc.dram_tensor("rw_" + n, [S, 1024], F32)
    k.bon_d = nc.dram_tensor("bon_d", [S, 16], F32).ap()

    NW = 51 * 1024 + 512
    with ExitStack() as es:
        es.enter_context(nc.allow_non_contiguous_dma(reason="small param layouts"))
        arena_t = es.enter_context(nc.sbuf_tensor("arena", [128, NW], F32))
        k.arena = Arena(arena_t, NW)
        k.ps = [es.enter_context(nc.psum_tensor(f"psb{i}", [128, 512], F32))[:, :] for i in range(8)]
        k.bank = 0

        def next_bank():
            b = k.bank
            k.bank = (k.bank + 1) % 8
            return b
        k.next_bank = next_bank
        k.p = Prog(nc, es)
        p, a = k.p, k.arena
        ident_f = a.f32(128)
        k.ident_f = ident_f
        p.dma("sync", ident_f, din["c_ident"], writes=["ident_f"])
        k.ident_bf = a.bf16(128)
        p.op("vector", lambda E: E.tensor_copy(out=k.ident_bf, in_=ident_f), reads=["ident_f"], writes=["ident"])
        k.J = a.f32(128)
        p.dma("sync", k.J, din["c_J"], writes=["J"])
        k.eps_t = a.f32(4)
        p.op("vector", lambda E: E.memset(k.eps_t[:, 0:1], EPS), writes=["eps"])
        p.op("vector", lambda E: E.memset(k.eps_t[:, 1:2], 1e-5), writes=["eps"])
        p.op("vector", lambda E: E.memset(k.eps_t[:, 2:3], 64e-5), writes=["eps"])
        p.op("vector", lambda E: E.memset(k.eps_t[:, 3:4], 0.0), writes=["eps"])
        stages = dbg.get("stages", "PACBO")
        layers = dbg.get("layers", list(range(NL)))
        if "F" in stages or "A" in stages:
            setup_bias_table(k)
        k.uT_f32 = a.f32(16 * S // 2)
        k.uT = k.uT_f32.bitcast(BF16).rearrange("p (c t) -> p c t", c=16)
        for l in layers:
            xsrc = din["x"] if l == 0 else k.x1_d
            ydst = k.x1_d if l < NL - 1 else y
            if dbg.get("single_layer_out"):
                ydst = y
            if "P" in stages:
                stage_prenorm(k, l, xsrc)
            if "A" in stages:
                stage_attn(k, l, dbg.get("heads_a", range(8)))
            if "C" in stages:
                stage_hgrn(k, l, dbg.get("heads_c", range(8)))
            if "B" in stages:
                stage_rwkv(k, l)
            if "O" in stages:
                stage_out(k, l, xsrc, ydst)
        for name, fn in dbg.get("dumps", {}).items():
            fn(k, dout[name])
        p.finish()
        p.emit()
    return nc


def t5_bucket_np(rel):
    n = np.abs(rel)
    nf = np.maximum(n, 8).astype(np.float32)
    large = 8 + (np.log(nf / 8) / math.log(16) * 8).astype(np.int32)
    large = np.minimum(large, 15)
    return np.where(rel > 0, 16, 0) + np.where(n < 8, n, large)


def consts():
    c = {"c_ident": np.eye(128, dtype=np.float32)}
    c["c_J"] = np.ascontiguousarray(np.eye(128, dtype=np.float32)[::-1])
    r = np.arange(1280)
    bk = t5_bucket_np(639 - r)
    oh = np.zeros((32, 1280), np.float32)
    oh[bk, r] = 1.0
    c["c_oh"] = oh
    s_ = np.arange(128)[:, None]
    t_ = np.arange(128)[None, :]
    same = (s_ // 64) == (t_ // 64)
    c["c_maskF"] = (same & (s_ <= t_)).astype(np.float32)
    c["c_maskB"] = (same & (s_ >= t_)).astype(np.float32)
    t = np.arange(S)
    cm = np.zeros((3, S), np.float32)
    cm[0] = (t % 64 != 0)
    cm[1] = ((t // 64) % 2 == 0)
    cm[2] = ((t // 64) % 2 == 1)
    c["c_cm"] = np.ascontiguousarray(np.broadcast_to(cm.reshape(1, 3 * S), (128, 3 * S)))
    sel = np.zeros((128, 256), np.float32)
    sel[0:64, 0:128] = 1.0
    sel[64:128, 128:256] = 1.0
    c["c_sel"] = sel
    return c


def kernel(**inputs):
    n = 8
    nc = build()
    c = consts()
    in_maps = []
    for b in range(n):
        m = {kk: np.ascontiguousarray(v) for kk, v in inputs.items() if kk != "x"}
        m["x"] = np.ascontiguousarray(inputs["x"][b])
        m.update(c)
        in_maps.append(m)
    res = run_bass_kernel_spmd(nc, in_maps, core_ids=list(range(n)))
    return np.stack([np.asarray(r["y"]) for r in res.results], axis=0).astype(np.float32)
```
